# Optimizing a Trainium2 kernel written in Bass

```python
import math
import jax, jax.numpy as jnp
from jax import lax
import numpy as np

D_MODEL = 1024
BATCH = 4
SEQ = 4096
DEPTH = 1
DEC_BATCH = 32
DEC_SEQ = 32
PAST_LEN = 2048

CHUNK = 64
Q_BLOCK = 128
PLE_DIM = 256
EPS = 1e-6
A_HEADS = D_MODEL // 128
A_DIM = 64
A_VDIM = 2 * A_DIM
R_HEADS = D_MODEL // 128
R_DK = 64
R_DV = 128
ROPE_BASE = 10000.0
D_FF = -(-8 * D_MODEL // (3 * 256)) * 256
IN_WIDTHS = (
    A_HEADS * 2 * A_DIM,
    A_HEADS * 2 * A_DIM,
    A_HEADS * A_VDIM,
    R_HEADS * R_DK,
    R_HEADS * R_DK,
    R_HEADS * R_DV,
    R_HEADS * R_DV,
    D_MODEL,
    D_MODEL,
)
IN_WIDTH = sum(IN_WIDTHS)

kernel_name = "diffattn_retention_gated_stream_step"


def rmsnorm(x, g):
    xf = x.astype(jnp.float32)
    y = xf * lax.rsqrt(jnp.mean(xf * xf, axis=-1, keepdims=True) + EPS)
    return (y * g.astype(jnp.float32)).astype(x.dtype)


def rotary(x, pos):
    half = x.shape[-1] // 2
    inv_freq = ROPE_BASE ** (-jnp.arange(half, dtype=jnp.float32) / half)
    ang = pos.astype(jnp.float32)[:, None] * inv_freq[None, :]
    cos = jnp.cos(ang)[:, None, :]
    sin = jnp.sin(ang)[:, None, :]
    x1, x2 = x[..., :half], x[..., half:]
    return jnp.concatenate([x1 * cos - x2 * sin, x1 * sin + x2 * cos], axis=-1)


def mix_inputs(x, w_in, g_norm, g_qn, g_kn, pos):
    B, T, _ = x.shape
    h = rmsnorm(x, g_norm)
    z = h @ w_in
    splits = np.cumsum(IN_WIDTHS)[:-1].tolist()
    qa, ka, va, qr, kr, vr, g_ret, gate_a, gate_r = jnp.split(z, splits, axis=-1)
    qa = rmsnorm(qa.reshape(B, T, A_HEADS, 2, A_DIM), g_qn)
    ka = rmsnorm(ka.reshape(B, T, A_HEADS, 2, A_DIM), g_kn)
    va = va.reshape(B, T, A_HEADS, A_VDIM)
    qr = rotary(qr.reshape(B, T, R_HEADS, R_DK).astype(jnp.float32), pos)
    kr = rotary(kr.reshape(B, T, R_HEADS, R_DK).astype(jnp.float32), pos) * (R_DK ** -0.5)
    vr = vr.reshape(B, T, R_HEADS, R_DV).astype(jnp.float32)
    return qa, ka, va, qr, kr, vr, g_ret, gate_a, gate_r


def diff_attn_prompt(qa, ka, va, lam):
    B, S = qa.shape[:2]
    nb = S // Q_BLOCK
    scale = A_DIM ** -0.5
    k_chunk = jnp.arange(S) // CHUNK
    qb = qa.reshape(B, nb, Q_BLOCK, A_HEADS, 2, A_DIM).swapaxes(0, 1)

    def block(args):
        q_blk, bi = args
        s = jnp.einsum('bqhcd,bkhcd->bhcqk', q_blk, ka).astype(jnp.float32) * scale
        q_chunk = (bi * Q_BLOCK + jnp.arange(Q_BLOCK)) // CHUNK
        mask = k_chunk[None, :] <= q_chunk[:, None]
        p = jax.nn.softmax(jnp.where(mask, s, -jnp.inf), axis=-1)
        a = p[:, :, 0] - lam * p[:, :, 1]
        return jnp.einsum('bhqk,bkhe->bqhe', a.astype(va.dtype), va)

    o = lax.map(block, (qb, jnp.arange(nb)))
    return o.swapaxes(0, 1).reshape(B, S, A_HEADS, A_VDIM)


def diff_attn_sample(qa, k_all, v_all, lam):
    s = jnp.einsum('bqhcd,bkhcd->bhcqk', qa, k_all).astype(jnp.float32) * (A_DIM ** -0.5)
    p = jax.nn.softmax(s, axis=-1)
    a = p[:, :, 0] - lam * p[:, :, 1]
    return jnp.einsum('bhqk,bkhe->bqhe', a.astype(v_all.dtype), v_all)


def diff_out(o, g_sub, lam_init):
    B, T = o.shape[:2]
    return (rmsnorm(o, g_sub) * (1.0 - lam_init)).reshape(B, T, A_HEADS * A_VDIM)


def ret_decay(T):
    log_g = jnp.log1p(-jnp.exp2(-5.0 - jnp.arange(R_HEADS, dtype=jnp.float32)))
    i = jnp.arange(T, dtype=jnp.float32)
    diff = i[:, None] - i[None, :]
    d_mat = jnp.where(diff >= 0, jnp.exp(log_g[:, None, None] * jnp.maximum(diff, 0.0)), 0.0)
    q_decay = jnp.exp(log_g[None, :] * (i[:, None] + 1.0))
    k_decay = jnp.exp(log_g[None, :] * (T - 1.0 - i[:, None]))
    chunk_decay = jnp.exp(log_g * T)
    return d_mat, q_decay, k_decay, chunk_decay


def retention_chunk(q, k, v, s0, d_mat, q_decay):
    inner = jnp.einsum('...ihd,...jhd->...hij', q, k) * d_mat
    o = jnp.einsum('...hij,...jhe->...ihe', inner, v)
    return o + jnp.einsum('...ihd,...hde->...ihe', q, s0) * q_decay[:, :, None]


def ret_increment(k, v, k_decay):
    return jnp.einsum('...jhd,...jhe->...hde', k * k_decay[:, :, None], v)


def retention_prompt(q, k, v):
    B, S = q.shape[:2]
    nc = S // CHUNK
    d_mat, q_decay, k_decay, chunk_decay = ret_decay(CHUNK)
    qc = q.reshape(B, nc, CHUNK, R_HEADS, R_DK)
    kc = k.reshape(B, nc, CHUNK, R_HEADS, R_DK)
    vc = v.reshape(B, nc, CHUNK, R_HEADS, R_DV)
    kv = ret_increment(kc, vc, k_decay)

    def step(s, kv_c):
        return chunk_decay[:, None, None] * s + kv_c, s

    s0 = jnp.zeros((B, R_HEADS, R_DK, R_DV), jnp.float32)
    s_final, s_starts = lax.scan(step, s0, kv.swapaxes(0, 1))
    o = retention_chunk(qc, kc, vc, s_starts.swapaxes(0, 1), d_mat, q_decay)
    return o.reshape(B, S, R_HEADS, R_DV), s_final


def retention_sample(q, k, v, s0):
    T = q.shape[1]
    d_mat, q_decay, k_decay, chunk_decay = ret_decay(T)
    o = retention_chunk(q, k, v, s0, d_mat, q_decay)
    s_new = chunk_decay[:, None, None] * s0 + ret_increment(k, v, k_decay)
    return o, s_new


def ret_out(o, g_ret, g_rn):
    B, T = o.shape[:2]
    o = rmsnorm(o, g_rn).reshape(B, T, R_HEADS * R_DV).astype(g_ret.dtype)
    return o * jax.nn.silu(g_ret)


def merge_and_tail(x, o_a, o_r, gate_a, gate_r, p, w_o, g_ffn, w_ff_gate, w_ff_up,
                   w_ff_down, g_ple, w_ple, w_ple_gate):
    mix = jax.nn.sigmoid(gate_a) * o_a + jax.nn.sigmoid(gate_r) * o_r
    x = x + mix @ w_o
    h = rmsnorm(x, g_ffn)
    x = x + (jax.nn.silu(h @ w_ff_gate) * (h @ w_ff_up)) @ w_ff_down
    gate = jax.nn.sigmoid(rmsnorm(x, g_ple) @ w_ple_gate)
    return x + gate * (p @ w_ple)


def setup_inputs(seed: int = 0) -> dict:
    key = jax.random.key(seed)
    ks = jax.random.split(key, 24)
    f32 = jnp.float32

    def nrm(k, shape, s):
        return jax.random.normal(k, shape, f32) * s

    def gain(k, shape):
        return 1.0 + 0.02 * jax.random.normal(k, shape, f32)

    return {
        "x_prompt": nrm(ks[0], (BATCH, SEQ, D_MODEL), 1.0),
        "x_sample": nrm(ks[1], (DEC_BATCH, DEC_SEQ, D_MODEL), 1.0),
        "cache_attn_k": nrm(ks[2], (DEPTH, DEC_BATCH, PAST_LEN, A_HEADS, 2 * A_DIM), 1.0),
        "cache_attn_v": nrm(ks[3], (DEPTH, DEC_BATCH, PAST_LEN, A_HEADS, A_VDIM), 1.0),
        "state_ret": nrm(ks[4], (DEPTH, DEC_BATCH, R_HEADS, R_DK, R_DV), 0.5),
        "p_prompt": nrm(ks[5], (DEPTH, BATCH, SEQ, PLE_DIM), 1.0),
        "p_sample": nrm(ks[6], (DEPTH, DEC_BATCH, DEC_SEQ, PLE_DIM), 1.0),
        "w_in": nrm(ks[7], (DEPTH, D_MODEL, IN_WIDTH), D_MODEL ** -0.5),
        "g_mix_norm": gain(ks[8], (DEPTH, D_MODEL)),
        "g_q_norm": gain(ks[9], (DEPTH, A_DIM)),
        "g_k_norm": gain(ks[10], (DEPTH, A_DIM)),
        "lam_q": nrm(ks[11], (DEPTH, 2, A_DIM), 0.1),
        "lam_k": nrm(ks[12], (DEPTH, 2, A_DIM), 0.1),
        "g_sub_norm": gain(ks[13], (DEPTH, A_VDIM)),
        "g_ret_norm": gain(ks[14], (DEPTH, R_HEADS, R_DV)),
        "w_o": nrm(ks[15], (DEPTH, D_MODEL, D_MODEL), D_MODEL ** -0.5),
        "g_ffn_norm": gain(ks[16], (DEPTH, D_MODEL)),
        "w_ff_gate": nrm(ks[17], (DEPTH, D_MODEL, D_FF), D_MODEL ** -0.5),
        "w_ff_up": nrm(ks[18], (DEPTH, D_MODEL, D_FF), D_MODEL ** -0.5),
        "w_ff_down": nrm(ks[19], (DEPTH, D_FF, D_MODEL), D_FF ** -0.5),
        "g_ple_norm": gain(ks[20], (DEPTH, D_MODEL)),
        "w_ple": nrm(ks[21], (DEPTH, PLE_DIM, D_MODEL), PLE_DIM ** -0.5),
        "w_ple_gate": nrm(ks[22], (DEPTH, D_MODEL, D_MODEL), D_MODEL ** -0.5),
    }


def reference(x_prompt, x_sample, cache_attn_k, cache_attn_v, state_ret, p_prompt, p_sample,
              w_in, g_mix_norm, g_q_norm, g_k_norm, lam_q, lam_k, g_sub_norm, g_ret_norm, w_o,
              g_ffn_norm, w_ff_gate, w_ff_up, w_ff_down, g_ple_norm, w_ple, w_ple_gate):
    S = x_prompt.shape[1]
    T = x_sample.shape[1]
    P = cache_attn_k.shape[2]
    DB = x_sample.shape[0]
    pos_p = jnp.arange(S)
    pos_s = P + jnp.arange(T)
    xp, xs = x_prompt, x_sample
    kp_l, vp_l, sp_l, ks_l, vs_l, ss_l = [], [], [], [], [], []
    for l in range(DEPTH):
        lam_init = 0.8 - 0.6 * math.exp(-0.3 * l)
        lq = lam_q[l].astype(jnp.float32)
        lk = lam_k[l].astype(jnp.float32)
        lam = jnp.exp(jnp.sum(lq[0] * lk[0])) - jnp.exp(jnp.sum(lq[1] * lk[1])) + lam_init

        qa, ka, va, qr, kr, vr, g_ret, gate_a, gate_r = mix_inputs(
            xp, w_in[l], g_mix_norm[l], g_q_norm[l], g_k_norm[l], pos_p)
        o_a = diff_out(diff_attn_prompt(qa, ka, va, lam), g_sub_norm[l], lam_init)
        o_r, s_fin = retention_prompt(qr, kr, vr)
        o_r = ret_out(o_r, g_ret, g_ret_norm[l])
        kp_l.append(ka.reshape(xp.shape[0], S, A_HEADS, 2 * A_DIM).astype(cache_attn_k.dtype))
        vp_l.append(va.astype(cache_attn_v.dtype))
        sp_l.append(s_fin.astype(state_ret.dtype))
        xp = merge_and_tail(xp, o_a, o_r, gate_a, gate_r, p_prompt[l], w_o[l], g_ffn_norm[l],
                            w_ff_gate[l], w_ff_up[l], w_ff_down[l], g_ple_norm[l], w_ple[l],
                            w_ple_gate[l])

        qa, ka, va, qr, kr, vr, g_ret, gate_a, gate_r = mix_inputs(
            xs, w_in[l], g_mix_norm[l], g_q_norm[l], g_k_norm[l], pos_s)
        k_all = jnp.concatenate(
            [cache_attn_k[l].reshape(DB, P, A_HEADS, 2, A_DIM).astype(ka.dtype), ka], axis=1)
        v_all = jnp.concatenate([cache_attn_v[l].astype(va.dtype), va], axis=1)
        o_a = diff_out(diff_attn_sample(qa, k_all, v_all, lam), g_sub_norm[l], lam_init)
        o_r, s_new = retention_sample(qr, kr, vr, state_ret[l].astype(jnp.float32))
        o_r = ret_out(o_r, g_ret, g_ret_norm[l])
        ks_l.append(ka.reshape(DB, T, A_HEADS, 2 * A_DIM).astype(cache_attn_k.dtype))
        vs_l.append(va.astype(cache_attn_v.dtype))
        ss_l.append(s_new.astype(state_ret.dtype))
        xs = merge_and_tail(xs, o_a, o_r, gate_a, gate_r, p_sample[l], w_o[l], g_ffn_norm[l],
                            w_ff_gate[l], w_ff_up[l], w_ff_down[l], g_ple_norm[l], w_ple[l],
                            w_ple_gate[l])

    return (xp, xs, jnp.stack(kp_l), jnp.stack(vp_l), jnp.stack(sp_l),
            jnp.stack(ks_l), jnp.stack(vs_l), jnp.stack(ss_l))
```

```python
import math
import numpy as np
from contextlib import ExitStack
import concourse.bass as bass
import concourse.mybir as mybir
from concourse.bass_utils import run_bass_kernel_spmd

F32 = mybir.dt.float32
BF16 = mybir.dt.bfloat16
AF = mybir.ActivationFunctionType
ALU = mybir.AluOpType
AX = mybir.AxisListType

COMPUTE = ("pe", "act", "dve", "pool")
ENGS = ("pe", "act", "dve", "pool", "sp")

D = 1024
NH = 8
SEQ = 4096
PAST = 2048
DFF = 2816
NF = DFF // 128
PLE = 256
EPS = 1e-6
NOWN = 17
NOTH = 16
NLB = NOWN + NOTH
SAMP = 16
LAM_INIT = 0.8 - 0.6 * math.exp(-0.3 * 0)
CFG = dict(nheads=NH, niters=16, sample=True)
SAME_ENGINE_ALL = True
SCHED = True
DUR = {"pe": 100.0, "act": 380.0, "dve": 280.0, "pool": 450.0, "sp": 60.0}
SEM_LAT = 250.0
OK_, OKR, OV, OVR, OQ, OQR, OGRET, OGA, OGR = 0, 128, 192, 320, 448, 576, 640, 768, 896


class Buf:
    __slots__ = ("name", "w", "r")

    def __init__(self, name):
        self.name = name
        self.w = None
        self.r = []


class Prog:
    def __init__(self, nc):
        self.nc = nc
        self.streams = {e: [] for e in ENGS}
        self.marked = {e: set() for e in COMPUTE}
        self.dma_cum = {}
        self.bufs = {}
        self.final_tokens = []
        self.bank_last = {}
        self.bankmap = lambda name: ()
        self.cap = None
        self._atomic = 0
        self.eng_free = {e: 0.0 for e in ENGS}
        self.done = {}
        self.sched = SCHED

    def buf(self, *key):
        b = self.bufs.get(key)
        if b is None:
            b = Buf(key)
            self.bufs[key] = b
        return b

    def _deps(self, reads, writes, tok, eng=None):
        raw = set()
        oth = set()
        for b in reads:
            if b.w is not None:
                raw.add(b.w)
        for b in writes:
            if b.w is not None:
                oth.add(b.w)
            for t in b.r:
                oth.add(t)
        for b in reads:
            b.r.append(tok)
        for b in writes:
            b.w = tok
            b.r = []
        waits = set()
        for w in raw | oth:
            if w == tok:
                continue
            if eng is not None and w[0] == "c" and w[1] == eng:
                if eng == "pe" or (w not in raw and not SAME_ENGINE_ALL):
                    continue
            waits.add(w)
        return waits

    def _mark(self, waits):
        for w in waits:
            if w[0] == "c":
                self.marked[w[1]].add(w[2])

    def begin(self):
        self.cap = []
        self._atomic = 0

    def end(self):
        c, self.cap = self.cap, None
        return c

    def atomic(self):
        prog = self

        class _A:
            def __enter__(self_):
                if prog.cap is not None:
                    if prog._atomic == 0:
                        prog.cap.append([])
                    prog._atomic += 1

            def __exit__(self_, *a):
                if prog.cap is not None:
                    prog._atomic -= 1
                return False
        return _A()

    def _record(self, item):
        if self._atomic:
            self.cap[-1].append(item)
        else:
            self.cap.append([item])

    def _peek(self, item):
        kind, args, kw = item
        if kind == "op":
            eng, fn, reads, writes = args[0], args[1], args[2], args[3]
        else:
            eng, reads, writes = args[0], kw.get("reads", ()), kw.get("writes", ())
        t = self.eng_free[eng]
        toks = []
        for b in reads:
            if b.w is not None:
                toks.append(b.w)
        for b in writes:
            if b.w is not None:
                toks.append(b.w)
            toks.extend(b.r)
        if kind == "op":
            banks = set()
            for b in list(reads) + list(writes):
                banks |= set(self.bankmap(b.name))
            for bk in banks:
                for e2, t2 in self.bank_last.get(bk, {}).items():
                    if e2 != eng:
                        toks.append(t2)
        for tk in toks:
            d = self.done.get(tk, 0.0)
            if not (tk[0] == "c" and tk[1] == eng):
                d += SEM_LAT
            if d > t:
                t = d
        return t

    def interleave(self, lists):
        lists = [l for l in lists if l]
        pos = [0] * len(lists)
        while True:
            cand = [k for k, l in enumerate(lists) if pos[k] < len(l)]
            if not cand:
                break
            ratios = {k: pos[k] / len(lists[k]) for k in cand}
            if self.sched:
                rmin = min(ratios.values())
                tmin = None
                est = {}
                for k in cand:
                    est[k] = self._peek(lists[k][pos[k]][0])
                    tmin = est[k] if tmin is None else min(tmin, est[k])
                bk = min(cand, key=lambda k: (est[k] - tmin) + 4000.0 * (ratios[k] - rmin))
            else:
                bk = min(cand, key=lambda k: ratios[k])
            for (kind, args, kw) in lists[bk][pos[bk]]:
                if kind == "op":
                    self.op(*args, **kw)
                else:
                    self.dma(*args, **kw)
            pos[bk] += 1

    def op(self, eng, fn, reads=(), writes=(), cost=None):
        if self.cap is not None:
            self._record(("op", (eng, fn, tuple(reads), tuple(writes)), dict(cost=cost)))
            return None
        idx = len(self.streams[eng])
        tok = ("c", eng, idx)
        waits = self._deps(reads, writes, tok, eng)
        banks = set()
        for b in list(reads) + list(writes):
            banks |= set(self.bankmap(b.name))
        for bk in banks:
            last = self.bank_last.setdefault(bk, {})
            for e2, t2 in last.items():
                if e2 != eng:
                    waits.add(t2)
            last[eng] = tok
        self._mark(waits)
        self.streams[eng].append(dict(kind="c", fn=fn, waits=waits))
        t = self.eng_free[eng]
        for w in waits:
            d = self.done.get(w, 0.0) + (0.0 if (w[0] == "c" and w[1] == eng) else SEM_LAT)
            if d > t:
                t = d
        t += (cost if cost is not None else DUR[eng])
        self.eng_free[eng] = t
        self.done[tok] = t
        return tok

    def dma(self, queue, pairs, key, reads=(), writes=(), final=False):
        if self.cap is not None:
            self._record(("dma", (queue, list(pairs), key), dict(reads=tuple(reads), writes=tuple(writes), final=final)))
            return None
        base = self.dma_cum.get(key, 0)
        cum = base + 16 * len(pairs)
        self.dma_cum[key] = cum
        tok = ("d", key, cum)
        waits = self._deps(reads, writes, tok)
        self._mark(waits)
        first = True
        for (o, i) in pairs:
            self.streams[queue].append(dict(kind="d", out=o, in_=i, key=key, waits=waits if first else set()))
            first = False
        if final:
            self.final_tokens.append(tok)
        t = self.eng_free[queue]
        for w in waits:
            d = self.done.get(w, 0.0) + SEM_LAT
            if d > t:
                t = d
        self.eng_free[queue] = t + DUR["sp"] * len(pairs)
        self.done[tok] = t + 2500.0
        return tok

    def barrier(self):
        toks = set()
        for e in COMPUTE:
            for idx in range(len(self.streams[e]) - 1, -1, -1):
                if self.streams[e][idx]["kind"] == "c":
                    toks.add(("c", e, idx))
                    break
        for k, cum in self.dma_cum.items():
            toks.add(("d", k, cum))
        self._mark(toks)
        for e in ENGS:
            self.streams[e].append(dict(kind="w", waits={t for t in toks if not (t[0] == "c" and t[1] == e)}))

    def emit(self, stack):
        nc = self.nc
        sems = {}
        for e in COMPUTE:
            sems[("c", e)] = stack.enter_context(nc.semaphore("s_" + e))
        for k in self.dma_cum:
            sems[("d", k)] = stack.enter_context(nc.semaphore("d_" + str(k)))
        tick = {}
        for e in COMPUTE:
            n = 0
            for idx in range(len(self.streams[e])):
                if self.streams[e][idx]["kind"] == "c" and idx in self.marked[e]:
                    n += 1
                    tick[(e, idx)] = n
        self.streams["sp"].append(dict(kind="w", waits=set(self.final_tokens)))
        block = stack.enter_context(nc.Block())
        handles = {"pe": block.tensor, "act": block.scalar, "dve": block.vector,
                   "pool": block.gpsimd, "sp": block.sync}
        stats = {}
        for e in ENGS:
            stream = self.streams[e]
            if not stream:
                continue
            nwait = [0]

            def body(eng, e=e, stream=stream, nwait=nwait):
                waited = {}
                for idx, o in enumerate(stream):
                    for w in sorted(o["waits"], key=str):
                        if w[0] == "c":
                            sk = ("c", w[1])
                            val = tick[(w[1], w[2])]
                        else:
                            sk = ("d", w[1])
                            val = w[2]
                        if waited.get(sk, 0) >= val:
                            continue
                        waited[sk] = val
                        eng.wait_ge(sems[sk], val)
                        nwait[0] += 1
                    if o["kind"] == "c":
                        ins = o["fn"](eng)
                        if idx in self.marked[e]:
                            ins.then_inc(sems[("c", e)], 1)
                    elif o["kind"] == "d":
                        eng.dma_start(out=o["out"], in_=o["in_"]).then_inc(sems[("d", o["key"])], 16)

            handles[e](body)
            stats[e] = (len(stream), nwait[0])
        return stats


def bc_rows(dram_ap_1d_tensor, offset, n, nparts=128):
    return bass.AP(dram_ap_1d_tensor, offset, [[0, nparts], [1, n]])


def build_program(debug=None, phases="ABC"):
    nc = bass.Bass("TRN2", target_bir_lowering=False)
    dbg_outs = {}

    def din(name, shape, dt=F32):
        return nc.dram_tensor(name, list(shape), dt, kind="ExternalInput").ap()

    def dout(name, shape, dt=F32):
        return nc.dram_tensor(name, list(shape), dt, kind="ExternalOutput").ap()

    x_own = din("x_own", [NOWN * 128, D])
    x_oth = din("x_oth", [NOTH * 128, D])
    p_own = din("p_own", [NOWN * 128, PLE])
    ck = din("ck", [4, PAST, D])
    cv = din("cv", [4, PAST, D])
    sr = din("sr", [4, NH, 64, 128])
    w_in = din("w_in", [D, NH, 1024])
    w_o = din("w_o", [D, D])
    w_g = din("w_g", [D, DFF])
    w_u = din("w_u", [D, DFF])
    w_d = din("w_d", [DFF, D])
    w_ple = din("w_ple", [PLE, D])
    w_pg = din("w_pg", [D, D])
    g_mix = din("g_mix", [1, D])
    g_ffn = din("g_ffn", [1, D])
    g_ple = din("g_ple", [1, D])
    g_q = din("g_q", [1, 64])
    g_k = din("g_k", [1, 64])
    g_sub = din("g_sub", [1, 128])
    g_rn = din("g_rn", [1, NH * 128])
    lam_q = din("lam_q", [1, 128])
    lam_k = din("lam_k", [1, 128])
    c_ident = din("c_ident", [128, 128])
    c_cos = din("c_cos", [128, NLB, 32])
    c_sin = din("c_sin", [128, NLB, 64])
    c_kdc = din("c_kdc", [128, NH])
    c_kdcs = din("c_kdcs", [128, NH, 4])
    c_qdc = din("c_qdc", [128, NH])
    c_qdcs = din("c_qdcs", [128, NH])
    c_dt = din("c_dt", [128, NH, 128])
    c_dts = din("c_dts", [128, NH, 128])
    c_chn = din("c_chn", [64, 48])
    c_obias = din("c_obias", [128, 1])
    c_sbias = din("c_sbias", [128, 4])

    y_own = dout("y_own", [NOWN * 128, D])
    k_own = dout("k_own", [NOWN * 128, D])
    v_own = dout("v_own", [NOWN * 128, D])
    ret_p = dout("ret_p", [NH, 64, 128])
    ret_s = dout("ret_s", [4, NH, 64, 128])

    with ExitStack() as top:
        P = Prog(nc)
        B = P.buf
        mem = {"tot": 0}

        def bankmap(name):
            k = name[0]
            if k == "pA":
                return (0,) if name[1] == 0 else (7,)
            if k in ("pB", "pKV", "pAT"):
                return (1,)
            if k in ("pC", "pOr"):
                return (2,)
            if k == "pOa":
                return (3,)
            if k == "pT":
                return (4,)
            if k == "pS":
                return (5 + name[1],)
            if k == "bank":
                return (name[1],)
            return ()
        P.bankmap = bankmap

        def sb(stack, name, shape, dt):
            n = 1
            for s in shape[1:]:
                n *= s
            mem["tot"] += n * (4 if dt == F32 else 2)
            mem.setdefault("log", []).append((name, n * (4 if dt == F32 else 2)))
            return stack.enter_context(nc.sbuf_tensor(name, list(shape), dt))

        psum = top.enter_context(nc.psum_tensor("psum", [128, 4096], F32))

        def bank(b, c0=0, c1=512):
            return psum[:, b * 512 + c0: b * 512 + c1]

        class Rot:
            def __init__(self, stack, name, n, shape, dt):
                self.tiles = [sb(stack, "%s%d" % (name, i), shape, dt) for i in range(n)]
                self.name = name
                self.n = n
                self.i = -1

            def next(self):
                self.i += 1
                s = self.i % self.n
                return self.tiles[s], B(self.name, s)

        def dbg(name, ap, shape, dt, reads):
            if debug is None or name not in debug:
                return
            o = dout("dbg_" + name, shape, dt)
            dbg_outs[name] = (shape, dt)
            P.dma("sp", [(o, ap)], "dbg_" + name, reads=reads, final=True)

        ident_f = sb(top, "ident_f", [128, 128], F32)
        ident = sb(top, "ident", [128, 128], BF16)
        epsT = sb(top, "epsT", [128, 1], F32)
        mixT = sb(top, "mixT", [128, NH, NOWN * 128], BF16)
        gvec = sb(top, "gvec", [128, D], F32)

        P.dma("sp", [(ident_f[:, :], c_ident)], "c0", writes=[B("ident_f")])
        P.op("dve", lambda e: e.tensor_copy(out=ident[:, :], in_=ident_f[:, :]), [B("ident_f")], [B("ident")])
        P.op("dve", lambda e: e.memset(epsT[:, :], EPS), [], [B("eps")])
        P.dma("sp", [(gvec[:, :], bc_rows(g_mix.tensor, 0, D))], "c1", writes=[B("gvec")])

        hT_scr = nc.dram_tensor("hT_scr", [NLB, 128, 8 * 128], BF16).ap()

        with ExitStack() as sA:
            xt_rot = Rot(sA, "xt", 3, [128, D], F32)
            hb_rot = Rot(sA, "hb", 2, [128, D], BF16)
            hs_rot = Rot(sA, "hs", 2, [128, D], BF16)
            junk = sb(sA, "junk", [128, D], BF16)

            def xblk(lb):
                return x_own[lb * 128:(lb + 1) * 128, :] if lb < NOWN else x_oth[(lb - NOWN) * 128:(lb - NOWN + 1) * 128, :]

            st_r = Rot(sA, "st1", 3, [128, 4], F32)
            order = []
            for i in range(16):
                order += [NOWN + i, i]
            order.append(SAMP)
            for n_, lb in enumerate(order):
                xt, xb = xt_rot.next()
                hb, hbb = hb_rot.next()
                hs, hsb = hs_rot.next()
                st1, st1b = st_r.next()
                P.dma("sp", [(xt[:, :], xblk(lb))], "xt%d" % (xt_rot.i % 3), writes=[xb])
                P.op("act", lambda e, xt=xt, st1=st1: e.activation(out=junk[:, :], in_=xt[:, :], func=AF.Square, accum_out=st1[:, 0:1]),
                     [xb], [B("junk"), st1b])
                P.op("act", lambda e, st1=st1: e.activation(out=st1[:, 1:2], in_=st1[:, 0:1], func=AF.Ln, scale=1.0 / D, bias=epsT[:, :]),
                     [st1b, B("eps")], [st1b])
                P.op("act", lambda e, st1=st1: e.activation(out=st1[:, 2:3], in_=st1[:, 1:2], func=AF.Exp, scale=-0.5), [st1b], [st1b])
                P.op("dve", lambda e, xt=xt, hb=hb, st1=st1: e.scalar_tensor_tensor(
                    out=hb[:, :], in0=xt[:, :], scalar=st1[:, 2:3], in1=gvec[:, :], op0=ALU.mult, op1=ALU.mult),
                    [xb, st1b, B("gvec")], [hbb])
                bk = 4 + (n_ % 2)
                pTa = bank(bk).bitcast(BF16)
                for kt in range(8):
                    P.op("pe", lambda e, hb=hb, kt=kt, pTa=pTa: e.transpose(out=pTa[:, kt * 128:(kt + 1) * 128],
                                                                             in_=hb[:, kt * 128:(kt + 1) * 128], identity=ident[:, :]),
                         [hbb, B("ident")], [B("bank", bk)])
                if n_ % 2 == 0:
                    P.op("act", lambda e, hs=hs, pTa=pTa: e.activation(out=hs[:, :], in_=pTa, func=AF.Copy), [B("bank", bk)], [hsb])
                else:
                    P.op("dve", lambda e, hs=hs, pTa=pTa: e.tensor_copy(out=hs[:, :], in_=pTa), [B("bank", bk)], [hsb])
                P.dma("pool", [(hT_scr[lb], hs[:, :])], "hs%d" % (hs_rot.i % 2), reads=[hsb], writes=[B("hTs", lb)])
        P.barrier()
        if "B" in phases:
            with ExitStack() as sB:
                phase_B(nc, P, B, sB, sb, Rot, bank, dbg, locals())
        if "C" in phases:
            P.barrier()
            with ExitStack() as sC:
                phase_C(nc, P, B, sC, sb, Rot, bank, dbg, locals())
        stats = P.emit(top)
        print("emit stats", stats, "sbuf bytes/partition (sum of all tiles)", mem["tot"], flush=True)
    return nc, dbg_outs


def phase_B(nc, P, B, sB, sb, Rot, bank, dbg, env):
    hT_scr = env["hT_scr"]
    psum = env["psum"]
    ident = env["ident"]
    epsT = env["epsT"]
    mixT = env["mixT"]
    w_in = env["w_in"]
    k_own, v_own, ret_p, ret_s = env["k_own"], env["v_own"], env["ret_p"], env["ret_s"]
    ck, cv, sr = env["ck"], env["cv"], env["sr"]

    cosT = sb(sB, "cosT", [128, NLB, 32], F32)
    sinT = sb(sB, "sinT", [128, NLB, 64], F32)
    kdc = sb(sB, "kdc", [128, NH], F32)
    kdcs = sb(sB, "kdcs", [128, NH, 4], F32)
    qdc = sb(sB, "qdc", [128, NH], F32)
    qdcs = sb(sB, "qdcs", [128, NH], F32)
    DT = sb(sB, "DT", [128, NH, 128], F32)
    DTs = sb(sB, "DTs", [128, NH, 128], F32)
    chn = sb(sB, "chn", [64, 48], F32)
    obias = sb(sB, "obias", [128, 1], F32)
    sbias = sb(sB, "sbias", [128, 4], F32)
    oneT = sb(sB, "oneT", [128, 1], F32)
    gq = sb(sB, "gq", [128, 64], F32)
    gk = sb(sB, "gk", [128, 64], F32)
    gsub = sb(sB, "gsub", [128, 128], F32)
    grn = sb(sB, "grn", [128, NH * 128], F32)
    lq = sb(sB, "lq", [128, 128], F32)
    lk = sb(sB, "lk", [128, 128], F32)
    lprod = sb(sB, "lprod", [128, 128], F32)
    lred = sb(sB, "lred", [128, 2], F32)
    lex = sb(sB, "lex", [128, 2], F32)
    neglam = sb(sB, "neglam", [128, 1], F32)
    CB = [B("constB")]
    P.dma("sp", [(cosT[:, :, :], env["c_cos"]), (sinT[:, :, :], env["c_sin"]), (kdc[:, :], env["c_kdc"]),
                 (kdcs[:, :, :], env["c_kdcs"]), (qdc[:, :], env["c_qdc"]), (qdcs[:, :], env["c_qdcs"]),
                 (DT[:, :, :], env["c_dt"]), (DTs[:, :, :], env["c_dts"]), (chn[:, :], env["c_chn"]),
                 (obias[:, :], env["c_obias"]), (sbias[:, :], env["c_sbias"]),
                 (gq[:, :], bc_rows(env["g_q"].tensor, 0, 64)), (gk[:, :], bc_rows(env["g_k"].tensor, 0, 64)),
                 (gsub[:, :], bc_rows(env["g_sub"].tensor, 0, 128)), (grn[:, :], bc_rows(env["g_rn"].tensor, 0, NH * 128)),
                 (lq[:, :], bc_rows(env["lam_q"].tensor, 0, 128)), (lk[:, :], bc_rows(env["lam_k"].tensor, 0, 128))],
          "c2", writes=CB)
    P.op("dve", lambda e: e.memset(oneT[:, :], 1.0), [], [B("oneT")])
    P.op("dve", lambda e: e.tensor_tensor(out=lprod[:, :], in0=lq[:, :], in1=lk[:, :], op=ALU.mult), CB, [B("lprod")])
    P.op("dve", lambda e: e.tensor_reduce(out=lred[:, :], in_=lprod[:, :].rearrange("p (c d) -> p c d", c=2), axis=AX.X, op=ALU.add),
         [B("lprod")], [B("lred")])
    P.op("act", lambda e: e.activation(out=lex[:, :], in_=lred[:, :], func=AF.Exp), [B("lred")], [B("lex")])
    P.op("dve", lambda e: e.tensor_tensor(out=neglam[:, :], in0=lex[:, 1:2], in1=lex[:, 0:1], op=ALU.subtract), [B("lex")], [B("neglam")])
    P.op("dve", lambda e: e.tensor_scalar(out=neglam[:, :], in0=neglam[:, :], scalar1=-LAM_INIT, scalar2=None, op0=ALU.add),
         [B("neglam")], [B("neglam")])
    P.op("dve", lambda e: e.tensor_scalar(out=gsub[:, :], in0=gsub[:, :], scalar1=1.0 - LAM_INIT, scalar2=None, op0=ALU.mult),
         CB, [B("gsubs")])

    def R(name, n, shape, dt):
        return Rot(sB, name, n, shape, dt)
    WA = R("WA", 2, [128, 8, 448], BF16)
    WB = R("WB", 2, [128, 8, 192], BF16)
    WC = R("WC", 2, [128, 8, 384], BF16)
    KT_t = sb(sB, "KT", [128, NLB * 128], BF16)
    V_t = sb(sB, "V", [128, NLB, 130], BF16)
    P.op("pool", lambda e: e.memset(V_t[:, :, 128:130], 1.0), [], [B("Vones")])
    Kc_rot = R("Kc", 2, [128, 16, 128], BF16)
    Vc_rot = R("Vc", 2, [128, 16, 130], BF16)
    for t in Vc_rot.tiles:
        P.op("pool", lambda e, t=t: e.memset(t[:, :, 128:130], 1.0), [], [B("Vones")])
    KcT = sb(sB, "KcT", [128, PAST], BF16)
    s0_rot = R("s0h", 2, [64, 4, 128], F32)
    hTA_rot = R("hTA", 6, [128, 8, 128], BF16)
    hTC_rot = R("hTC", 2, [128, 8, 128], BF16)

    zA_r = R("zA", 3, [128, 448], F32)
    zB_r = R("zB", 2, [128, 192], F32)
    sqj = sb(sB, "sqj", [128, 8, 128], BF16)
    sqn = [0]

    def junk(w):
        n_ = sqn[0] % 8
        sqn[0] += 1
        return sqj[:, n_, 0:w], B("sqj", n_)
    ss4_r = R("ss4", 3, [128, 4], F32)
    ln4_r = R("ln4", 3, [128, 4], F32)
    rs4_r = R("rs4", 3, [128, 4], F32)
    knf_r = R("knf", 2, [128, 128], F32)
    knb_r = R("knb", 3, [128, 128], BF16)
    qnb_r = R("qnb", 2, [128, 128], BF16)
    rt1_r = R("rt1", 2, [128, 128], F32)
    rt2_r = R("rt2", 2, [128, 128], F32)
    rot_r = R("rot", 2, [128, 128], F32)
    kdec_r = R("kdec", 4, [128, 4, 64], BF16)
    rbf_r = R("rbf", 2, [128, 192], BF16)
    vr_r = R("vrb", 6, [128, 128], BF16)
    tr_r = R("trT", 3, [64, 3, 128], BF16)
    qds_r = R("qds", 1, [64, 4, 128], BF16)
    QT_r = R("QTz", 3, [128, 2, 128], BF16)
    AT_r = R("AT", 2, [128, 128], BF16)
    KV_r = R("KV", 8, [64, 128], F32)
    S_r = R("S", 4, [64, 128], F32)
    St_r = R("St", 2, [64, 128], F32)
    Sb_r = R("Sbf", 2, [64, 128], BF16)
    SbS = sb(sB, "SbS", [64, 4, 128], BF16)
    PT_r = R("PT", 3, [128, 512], BF16)
    PTd = sb(sB, "PTd", [128, 2, 128], BF16)
    oasb_r = R("oasb", 2, [128, 258], F32)
    orsb_r = R("orsb", 2, [128, 128], F32)
    mixb_r = R("mixb", 2, [128, 128], BF16)
    P.op("pool", lambda e: e.memset(PTd[:, :, :], 0.0), [], [B("PTd")])
    for t in qds_r.tiles:
        P.op("pool", lambda e, t=t: e.memset(t[:, :, :], 0.0), [], [B("qds", 0)])
    for ti, t in enumerate(QT_r.tiles):
        P.op("pool", lambda e, t=t: e.memset(t[:, :, :], 0.0), [], [B("QTz", ti)])
    cm = {}
    for nm, shape in [("rl", [128, 2]), ("r2l", [128, 1]), ("o1", [128, 128]), ("oa", [128, 128]),
                      ("ssn", [128, 2]), ("lnn", [128, 2]), ("rsn", [128, 2]), ("An", [128, 128]), ("Rn", [128, 128]),
                      ("gret", [128, 128]), ("E", [128, 384]), ("L", [128, 384]), ("Sg", [128, 384]),
                      ("T1", [128, 128]), ("T2", [128, 128]), ("T3", [128, 128]), ("T4", [128, 128])]:
        cm[nm] = sb(sB, "cm_" + nm, shape, F32)

    pA_by = [bank(0, 0, 448), bank(7, 0, 448)]
    pB = bank(1, 0, 192)
    pKV = bank(1, 192, 320)
    pAT = bank(1, 320, 448)
    pC = bank(2, 0, 384)
    pOr = bank(2, 384, 512)
    pOa = bank(3, 0, 258)
    pT = bank(4).bitcast(BF16)
    pS = [bank(5), bank(6)]
    tslot = [0]
    sset = [0]

    def transpose_to(src_ap, src_bufs, rows_out, dsts):
        s = tslot[0] % 8
        tslot[0] += 1
        pt = pT[0:rows_out, s * 128:(s + 1) * 128]
        with P.atomic():
            P.op("pe", lambda e: e.transpose(out=pt, in_=src_ap, identity=ident[:, :]), list(src_bufs) + [B("ident")], [B("pT", s)])
            for (dst_ap, sel, dst_bufs) in dsts:
                P.op("dve", lambda e, dst_ap=dst_ap, sel=sel: e.tensor_copy(out=dst_ap, in_=sel(pt)), [B("pT", s)], dst_bufs)

    H = {}

    def load_weights(h):
        wa, wab = WA.next()
        wb, wbb = WB.next()
        wc, wcb = WC.next()
        src = w_in[:, h, :].rearrange("(kt p) c -> p kt c", p=128)
        P.dma("pool", [(wa[:, 0:4, :], src[:, 0:4, 0:448]), (wa[:, 4:8, :], src[:, 4:8, 0:448])], "WA%d" % (h % 2), writes=[wab])
        P.dma("pool", [(wb[:, :, :], src[:, :, 448:640])], "WB%d" % (h % 2), writes=[wbb])
        P.dma("pool", [(wc[:, 0:4, :], src[:, 0:4, 640:1024]), (wc[:, 4:8, :], src[:, 4:8, 640:1024])], "WC%d" % (h % 2), writes=[wcb])
        H.setdefault(h, {})["w"] = (wa, wab, wb, wbb, wc, wcb)

    cslots = {}

    def load_cache(h, s):
        Kc, Kcb = Kc_rot.next()
        Vc, Vcb = Vc_rot.next()
        ksrc = ck[s, :, h * 128:(h + 1) * 128].rearrange("(b p) d -> p b d", p=128)
        vsrc = cv[s, :, h * 128:(h + 1) * 128].rearrange("(b p) d -> p b d", p=128)
        P.dma("pool", [(Kc[:, :, :], ksrc)], "Kc%d" % (Kc_rot.i % 2), writes=[Kcb])
        P.dma("pool", [(Vc[:, :, 0:128], vsrc)], "Vc%d" % (Vc_rot.i % 2), writes=[Vcb])
        cslots[(h, s)] = (Kc, Kcb, Vc, Vcb)

    heads = list(range(CFG["nheads"]))
    nit = CFG["niters"]
    its = list(range(nit)) + ([SAMP] if CFG["sample"] else [])
    seq = [(h, i) for h in heads for i in its]
    hTA = {}

    def fetch_hTA(h, i):
        lbs = [SAMP] if i == SAMP else [NOWN + i, i]
        for lb in lbs:
            t, tb = hTA_rot.next()
            P.dma("sp", [(t[:, :, :], hT_scr[lb].rearrange("p (k t) -> p k t", k=8))], "hTA%d" % (hTA_rot.i % 6),
                  reads=[B("hTs", lb)], writes=[tb])
            hTA[(h, lb)] = (t, tb)

    def KTb(lb):
        return B("KT", lb)

    def Vb(lb):
        return B("V", lb)

    def proj(h, lb, own, st):
        wa, wab, wb, wbb, wc, wcb = H[h]["w"]
        tok = slice(lb * 128, (lb + 1) * 128)
        samp = (lb == SAMP)
        hTt, hTtb = hTA[(h, lb)]
        zA, zAb = zA_r.next()
        ss4, ss4b = ss4_r.next()
        ln4, ln4b = ln4_r.next()
        rs4, rs4b = rs4_r.next()
        pA = pA_by[1 if own else 0]
        pAb = B("pA", 1 if own else 0)
        for kt in range(8):
            P.op("pe", lambda e, kt=kt: e.matmul(pA, lhsT=hTt[:, kt, :], rhs=wa[:, kt, :], start=(kt == 0), stop=(kt == 7)),
                 [hTtb, wab], [pAb])
        P.op("act", lambda e: e.activation(out=zA[:, :], in_=pA, func=AF.Copy), [pAb], [zAb])
        for g in range(2):
            jk, jkb = junk(64)
            P.op("act", lambda e, g=g, jk=jk: e.activation(out=jk, in_=pA[:, g * 64:(g + 1) * 64], func=AF.Square,
                                                          accum_out=ss4[:, g:g + 1]), [pAb], [jkb, B("ss4", ss4_r.i % 3, g)])
        zB, zBb = (None, None)
        if own:
            zB, zBb = zB_r.next()
            with P.atomic():
                for kt in range(8):
                    P.op("pe", lambda e, kt=kt: e.matmul(pB, lhsT=hTt[:, kt, :], rhs=wb[:, kt, :], start=(kt == 0), stop=(kt == 7)),
                         [hTtb, wbb], [B("pB")])
                P.op("act", lambda e: e.activation(out=zB[:, :], in_=pB, func=AF.Copy), [B("pB")], [zBb])
                for g in range(2):
                    jk, jkb = junk(64)
                    P.op("act", lambda e, g=g, jk=jk: e.activation(out=jk, in_=pB[:, g * 64:(g + 1) * 64], func=AF.Square,
                                                                  accum_out=ss4[:, 2 + g:3 + g]), [B("pB")], [jkb, B("ss4", ss4_r.i % 3, 2 + g)])
        ng = 4 if own else 2
        P.op("act", lambda e: e.activation(out=ln4[:, 0:ng], in_=ss4[:, 0:ng], func=AF.Ln, scale=1.0 / 64, bias=epsT[:, :]),
             [B("ss4", ss4_r.i % 3, g_) for g_ in range(ng)] + [B("eps")], [ln4b])
        P.op("act", lambda e: e.activation(out=rs4[:, 0:ng], in_=ln4[:, 0:ng], func=AF.Exp, scale=-0.5), [ln4b], [rs4b])
        P.op("pool", lambda e: e.tensor_copy(out=V_t[:, lb, 0:128], in_=zA[:, OV:OV + 128]), [zAb], [Vb(lb)])
        vrb, vrbb = vr_r.next()
        P.op("pool", lambda e: e.tensor_copy(out=vrb[:, :], in_=zA[:, OVR:OVR + 128]), [zAb], [vrbb])
        rt1, rt1b = rt1_r.next()
        rt2, rt2b = rt2_r.next()
        rot, rotb = rot_r.next()
        cos_bc = bass.AP(cosT, lb * 32, [[NLB * 32, 128], [0, 2], [1, 32]])
        sin_ap = sinT[:, lb, :]

        def rotary(src, src_bufs, c0):
            sw = bass.AP(src.tensor, src.offset + 32, [list(src.ap[0]), [-32, 2], [1, 32]])
            P.op("dve", lambda e: e.tensor_tensor(out=rt1[:, c0:c0 + 64].rearrange("p (a b) -> p a b", a=2),
                                                  in0=src.rearrange("p (a b) -> p a b", a=2), in1=cos_bc, op=ALU.mult),
                 src_bufs + [B("constB")], [B("rt1h", rt1_r.i % 2, c0)])
            P.op("pool", lambda e: e.tensor_tensor(out=rt2[:, c0:c0 + 64].rearrange("p (a b) -> p a b", a=2), in0=sw,
                                                   in1=sin_ap.rearrange("p (a b) -> p a b", a=2), op=ALU.mult),
                 src_bufs + [B("constB")], [B("rt2h", rt2_r.i % 2, c0)])
        rotary(zA[:, OKR:OKR + 64], [zAb], 0)
        if own:
            rotary(zB[:, 128:192], [zBb], 64)
        nr = 128 if own else 64
        P.op("dve", lambda e: e.tensor_tensor(out=rot[:, 0:nr], in0=rt1[:, 0:nr], in1=rt2[:, 0:nr], op=ALU.add),
             [B("rt1h", rt1_r.i % 2, c_) for c_ in ((0, 64) if own else (0,))] + [B("rt2h", rt2_r.i % 2, c_) for c_ in ((0, 64) if own else (0,))], [rotb])
        knb, knbb = knb_r.next()
        if own:
            knf, knfb = knf_r.next()
            for g in range(2):
                P.op("dve", lambda e, g=g: e.scalar_tensor_tensor(out=knf[:, g * 64:(g + 1) * 64], in0=zA[:, g * 64:(g + 1) * 64],
                                                                  scalar=rs4[:, g:g + 1], in1=gk[:, :], op0=ALU.mult, op1=ALU.mult),
                     [zAb, rs4b, B("constB")], [B("knfh", knf_r.i % 2, g)])
            knfh = [B("knfh", knf_r.i % 2, g) for g in range(2)]
            P.dma("sp", [(k_own[tok, h * 128:(h + 1) * 128], knf[:, :])], "ko%d" % (knf_r.i % 2), reads=knfh, final=True)
            knbh = [B("knbh", knb_r.i % 3, g) for g in range(2)]
            P.op("pool", lambda e: e.tensor_copy(out=knb[:, :], in_=knf[:, :]), knfh, knbh)
            P.dma("sp", [(v_own[tok, h * 128:(h + 1) * 128], zA[:, OV:OV + 128])], "vo%d" % (zA_r.i % 3), reads=[zAb], final=True)
        else:
            for g in range(2):
                P.op("dve", lambda e, g=g: e.scalar_tensor_tensor(out=knb[:, g * 64:(g + 1) * 64], in0=zA[:, g * 64:(g + 1) * 64],
                                                                  scalar=rs4[:, g:g + 1], in1=gk[:, :], op0=ALU.mult, op1=ALU.mult),
                     [zAb, rs4b, B("constB")], [B("knbh", knb_r.i % 3, g)])
            knbh = [B("knbh", knb_r.i % 3, g) for g in range(2)]
        transpose_to(knb[:, :], knbh, 128, [(KT_t[:, tok], lambda pt: pt, [KTb(lb)])])
        kdec, kdecb = kdec_r.next()
        kvs = []
        nstream = 4 if samp else 1
        for s in range(nstream):
            sc = kdcs[:, h, s:s + 1] if samp else kdc[:, h:h + 1]
            P.op("pool", lambda e, s=s, sc=sc: e.tensor_scalar(out=kdec[:, s, :], in0=rot[:, 0:64], scalar1=sc, scalar2=0.0,
                                                                op0=ALU.mult, op1=ALU.add), [rotb, B("constB")], [kdecb])
        for s in range(nstream):
            KV, KVb = KV_r.next()
            with P.atomic():
                P.op("pe", lambda e, s=s: e.matmul(pKV[0:64, :], lhsT=kdec[:, s, :], rhs=vrb[:, :], start=True, stop=True),
                     [kdecb, vrbb], [B("pKV")])
                P.op("dve", lambda e, KV=KV: e.tensor_copy(out=KV[:, :], in_=pKV[0:64, :]), [B("pKV")], [KVb])
            kvs.append((KV, KVb))
        st["kv_" + ("own" if own else "oth")] = kvs
        if not own:
            return
        st["vrb"] = (vrb, vrbb)
        qnb, qnbb = qnb_r.next()
        for g in range(2):
            P.op("dve", lambda e, g=g: e.scalar_tensor_tensor(out=qnb[:, g * 64:(g + 1) * 64], in0=zB[:, g * 64:(g + 1) * 64],
                                                              scalar=rs4[:, 2 + g:3 + g], in1=gq[:, :], op0=ALU.mult, op1=ALU.mult),
                 [zBb, rs4b, B("constB")], [B("qnbh", qnb_r.i % 2, g)])
        QT, QTb = QT_r.next()
        transpose_to(qnb[:, :], [B("qnbh", qnb_r.i % 2, g) for g in range(2)], 128, [(QT[0:64, 0, :], lambda pt: pt[0:64, :], [QTb]),
                                              (QT[64:128, 1, :], lambda pt: pt[64:128, :], [QTb])])
        st["QT"] = (QT, QTb)
        rbf, rbfb = rbf_r.next()
        P.op("pool", lambda e: e.tensor_copy(out=rbf[:, 0:128], in_=rot[:, 0:128]), [rotb], [rbfb])
        qd_ap = (qdcs if samp else qdc)[:, h:h + 1]
        P.op("pool", lambda e: e.tensor_scalar(out=rbf[:, 128:192], in0=rot[:, 64:128], scalar1=qd_ap, scalar2=0.0, op0=ALU.mult, op1=ALU.add),
             [rotb, B("constB")], [rbfb])
        trT, trTb_ = tr_r.next()
        ti = tr_r.i % 3
        for j in range(2):
            transpose_to(rbf[:, j * 64:(j + 1) * 64], [rbfb], 64, [(trT[:, j, :], lambda pt: pt, [B("trT", ti, j)])])
        if samp:
            qds, qdsb = qds_r.next()
            transpose_to(rbf[:, 128:192], [rbfb], 64,
                         [(qds[:, s, s * 32:(s + 1) * 32], (lambda pt, s=s: pt[:, s * 32:(s + 1) * 32]), [qdsb]) for s in range(4)])
            st["qds"] = (qds, qdsb)
        else:
            transpose_to(rbf[:, 128:192], [rbfb], 64, [(trT[:, 2, :], lambda pt: pt, [B("trT", ti, 2)])])
        st["trT"] = (trT, [B("trT", ti, j) for j in range(3)])

    def stA(h, i, st, nxt):
        if i == its[0]:
            s0h, s0hb = s0_rot.next()
            P.dma("sp", [(s0h[:, :, :], sr[:, h, :, :].rearrange("s d e -> d s e"))], "s0h%d" % (s0_rot.i % 2), writes=[s0hb])
            H[h]["s0"] = (s0h, s0hb)
            H[h]["S"] = None
        if nxt is not None:
            fetch_hTA(*nxt)
        if len(its) > 3 and i == its[3] and h + 1 < len(heads):
            load_weights(h + 1)
        if i == SAMP:
            proj(h, SAMP, True, st)
        else:
            proj(h, NOWN + i, False, st)

    def stA2(h, i, st):
        if i != SAMP:
            proj(h, i, True, st)

    def chain(h, a_col, b_col, KV, KVb):
        S_new, S_newb = S_r.next()
        cur = H[h]["S"]
        b_ap = chn[:, b_col:b_col + 1]
        if cur is None:
            P.op("dve", lambda e: e.tensor_scalar(out=S_new[:, :], in0=KV[:, :], scalar1=b_ap, scalar2=None, op0=ALU.mult),
                 [KVb, B("constB")], [S_newb])
        else:
            So, Sob = cur
            St, Stb = St_r.next()
            a_ap = chn[:, a_col:a_col + 1]
            P.op("pool", lambda e: e.tensor_scalar(out=St[:, :], in0=So[:, :], scalar1=a_ap, scalar2=0.0, op0=ALU.mult, op1=ALU.add),
                 [Sob, B("constB")], [Stb])
            P.op("dve", lambda e: e.scalar_tensor_tensor(out=S_new[:, :], in0=KV[:, :], scalar=b_ap, in1=St[:, :], op0=ALU.mult, op1=ALU.add),
                 [KVb, Stb, B("constB")], [S_newb])
        H[h]["S"] = (S_new, S_newb)

    def pv(first, PT_ap_fn, ptb, lbs, Vsrc, last):
        for kbi, lb in enumerate(lbs):
            for m in range(2):
                stt = first[0]
                first[0] = False
                fin = last and kbi == len(lbs) - 1 and m == 1
                P.op("pe", lambda e, kbi=kbi, lb=lb, m=m, stt=stt, fin=fin: e.matmul(
                    pOa[:, m * 129:(m + 1) * 129], lhsT=PT_ap_fn(kbi, m), rhs=V_t[:, lb, 0:129], start=stt, stop=fin,
                    skip_group_check=True), [ptb, Vb(lb), B("Vones")], [B("pOa")])

    def prompt_attention(h, i, QT, QTb):
        keys = []
        for j in range(i):
            keys.append(j)
            keys.append(NOWN + j)
        first = [True]
        groups = [keys[g:g + 2] for g in range(0, len(keys), 2)] + [[NOWN + i, i]]
        G = len(groups)
        info = [None] * G
        QT2 = QT[:, :, :].rearrange("p m q -> p (m q)")

        def S(g):
            set_ = sset[0] % 2
            sset[0] += 1
            PT, PTb = PT_r.next()
            info[g] = (set_, PT, PTb)
            for kbi, lb in enumerate(groups[g]):
                P.op("pe", lambda e, kbi=kbi, lb=lb: e.matmul(
                    pS[set_][:, kbi * 256:(kbi + 1) * 256], lhsT=KT_t[:, lb * 128:(lb + 1) * 128], rhs=QT2, start=True, stop=True),
                    [KTb(lb), QTb], [B("pS", set_)])

        def E(g):
            set_, PT, PTb = info[g]
            if g < G - 1:
                n = len(groups[g])
                P.op("act", lambda e: e.activation(out=PT[:, 0:n * 256], in_=pS[set_][:, 0:n * 256], func=AF.Exp, scale=0.125),
                     [B("pS", set_)], [PTb])
            else:
                P.op("act", lambda e: e.activation(out=PT[:, 0:256], in_=pS[set_][:, 0:256], func=AF.Exp, scale=0.125, bias=obias[:, :]),
                     [B("pS", set_), B("constB")], [PTb])
                P.op("act", lambda e: e.activation(out=PTd[0:64, :, :], in_=pS[set_][0:64, 256:512].rearrange("p (m c) -> p m c", m=2),
                                                   func=AF.Exp, scale=0.125), [B("pS", set_)], [B("PTd")])
                P.op("act", lambda e: e.activation(out=PTd[64:128, :, 64:128],
                                                   in_=pS[set_][64:128, 256:512].rearrange("p (m c) -> p m c", m=2)[:, :, 64:128],
                                                   func=AF.Exp, scale=0.125), [B("pS", set_)], [B("PTd")])

        def Pv(g):
            set_, PT, PTb = info[g]
            if g < G - 1:
                pv(first, lambda kbi, m: PT[:, kbi * 256 + m * 128: kbi * 256 + (m + 1) * 128], PTb, groups[g], None, False)
            else:
                pv(first, lambda kbi, m: PT[:, m * 128:(m + 1) * 128], PTb, [NOWN + i], None, False)
                pv(first, lambda kbi, m: PTd[:, m, :], B("PTd"), [i], None, True)

        for g in range(G + 2):
            if g < G:
                S(g)
            if 1 <= g <= G:
                E(g - 1)
            if g >= 2:
                Pv(g - 2)

    def sample_stream(h, s, QT, QTb):
        first = [True]
        Kc, Kcb, Vc, Vcb = cslots[(h, s)]
        for b4 in range(4):
            tslot[0] = (tslot[0] + 3) // 4 * 4
            s0 = tslot[0] % 8
            tslot[0] += 4
            with P.atomic():
                for j in range(4):
                    bb = b4 * 4 + j
                    P.op("pe", lambda e, bb=bb, j=j, s0=s0: e.transpose(out=pT[:, (s0 + j) * 128:(s0 + j + 1) * 128], in_=Kc[:, bb, :], identity=ident[:, :]),
                         [Kcb, B("ident")], [B("pT", s0 + j)])
                P.op("dve", lambda e, b4=b4, s0=s0: e.tensor_copy(out=KcT[:, b4 * 512:(b4 + 1) * 512], in_=pT[:, s0 * 128:(s0 + 4) * 128]),
                     [B("pT", s0 + j) for j in range(4)], [B("KcT", b4)])
        q_sl = slice(s * 32, (s + 1) * 32)
        QTs = QT[:, :, q_sl]
        info = [None] * 3

        def S(half):
            set_ = sset[0] % 2
            sset[0] += 1
            PT, PTb = PT_r.next()
            info[half] = (set_, PT, PTb)
            if half < 2:
                for kbi in range(8):
                    b = half * 8 + kbi
                    P.op("pe", lambda e, kbi=kbi, b=b: e.matmul(
                        pS[set_][:, kbi * 64:(kbi + 1) * 64], lhsT=KcT[:, b * 128:(b + 1) * 128], rhs=QTs, start=True, stop=True),
                        [B("KcT", b // 4), QTb], [B("pS", set_)])
            else:
                P.op("pe", lambda e: e.matmul(pS[set_][:, 0:64], lhsT=KT_t[:, SAMP * 128:(SAMP + 1) * 128], rhs=QTs, start=True, stop=True),
                     [KTb(SAMP), QTb], [B("pS", set_)])

        def E(half):
            set_, PT, PTb = info[half]
            if half < 2:
                P.op("act", lambda e: e.activation(out=PT[:, :], in_=pS[set_], func=AF.Exp, scale=0.125), [B("pS", set_)], [PTb])
            else:
                P.op("act", lambda e: e.activation(out=PT[:, 0:64], in_=pS[set_][:, 0:64], func=AF.Exp, scale=0.125, bias=sbias[:, s:s + 1]),
                     [B("pS", set_), B("constB")], [PTb])

        def Pv(half):
            set_, PT, PTb = info[half]
            if half < 2:
                for kbi in range(8):
                    b = half * 8 + kbi
                    for m in range(2):
                        stt = first[0]
                        first[0] = False
                        P.op("pe", lambda e, kbi=kbi, b=b, m=m, stt=stt: e.matmul(
                            pOa[s * 32:(s + 1) * 32, m * 129:(m + 1) * 129], lhsT=PT[:, kbi * 64 + m * 32: kbi * 64 + (m + 1) * 32],
                            rhs=Vc[:, b, 0:129], start=stt, stop=False, skip_group_check=True, tile_position=(0, s * 32)),
                            [PTb, Vcb, B("Vones")], [B("pOa")])
            else:
                for m in range(2):
                    P.op("pe", lambda e, m=m: e.matmul(
                        pOa[s * 32:(s + 1) * 32, m * 129:(m + 1) * 129], lhsT=PT[:, m * 32:(m + 1) * 32],
                        rhs=V_t[:, SAMP, 0:129], start=False, stop=(m == 1), skip_group_check=True, tile_position=(0, s * 32)),
                        [PTb, Vb(SAMP), B("Vones")], [B("pOa")])

        S(0)
        S(1)
        E(0)
        S(2)
        E(1)
        Pv(0)
        E(2)
        Pv(1)
        Pv(2)
        if s + 2 < 4:
            load_cache(h, s + 2)

    def stB(h, i, st):
        wa, wab, wb, wbb, wc, wcb = H[h]["w"]
        samp = (i == SAMP)
        lb = i
        t, tb = hTC_rot.next()
        P.dma("sp", [(t[:, :, :], hT_scr[lb].rearrange("p (k t) -> p k t", k=8))], "hTC%d" % (hTC_rot.i % 2),
              reads=[B("hTs", lb)], writes=[tb])
        st["hTC"] = (t, tb)
        if (not samp) and CFG["sample"] and i == its[-2]:
            load_cache(h, 0)
            load_cache(h, 1)
        trT, trb = st["trT"]
        vrb, vrbb = st["vrb"]
        QT, QTb = st["QT"]
        P.op("pe", lambda e: e.matmul(pAT, lhsT=trT[:, 0, :], rhs=trT[:, 1, :], start=True, stop=True), [trb[0], trb[1]], [B("pAT")])
        AT, ATb = AT_r.next()
        dtab = (DTs if samp else DT)[:, h, :]
        P.op("dve", lambda e: e.tensor_tensor(out=AT[:, :], in0=pAT, in1=dtab, op=ALU.mult), [B("pAT"), B("constB")], [ATb])
        orsb, orsbb = orsb_r.next()
        if samp:
            s0h, s0hb = H[h]["s0"]
            qds, qdsb = st["qds"]
            P.op("pool", lambda e: e.tensor_copy(out=SbS[:, :, :], in_=s0h[:, :, :]), [s0hb], [B("SbS")])
            with P.atomic():
                for s in range(4):
                    P.op("pe", lambda e, s=s: e.matmul(pOr, lhsT=qds[:, s, :], rhs=SbS[:, s, :], start=(s == 0), stop=False),
                         [qdsb, B("SbS")], [B("pOr")])
                P.op("pe", lambda e: e.matmul(pOr, lhsT=AT[:, :], rhs=vrb[:, :], start=False, stop=True), [ATb, vrbb], [B("pOr")])
            for s in range(4):
                KV, KVb = st["kv_own"][s]
                So_, Sob_ = S_r.next()
                P.op("dve", lambda e, s=s, So_=So_, KV=KV: e.scalar_tensor_tensor(out=So_[:, :], in0=s0h[:, s, :], scalar=chn[:, 40 + h:41 + h],
                                                                                  in1=KV[:, :], op0=ALU.mult, op1=ALU.add),
                     [KVb, B("constB"), s0hb], [Sob_])
                P.dma("sp", [(ret_s[s, h, :, :], So_[:, :])], "rs%d" % (S_r.i % 4), reads=[Sob_], final=True)
        else:
            KVo, KVob = st["kv_oth"][0]
            KVn, KVnb = st["kv_own"][0]
            chain(h, 0 + h, 8 + h, KVo, KVob)
            Sa, Sab = H[h]["S"]
            Sbf, Sbfb = Sb_r.next()
            P.op("pool", lambda e: e.tensor_copy(out=Sbf[:, :], in_=Sa[:, :]), [Sab], [Sbfb])
            with P.atomic():
                P.op("pe", lambda e: e.matmul(pOr, lhsT=AT[:, :], rhs=vrb[:, :], start=True, stop=False), [ATb, vrbb], [B("pOr")])
                P.op("pe", lambda e: e.matmul(pOr, lhsT=trT[:, 2, :], rhs=Sbf[:, :], start=False, stop=True), [trb[2], Sbfb], [B("pOr")])
            S_new, S_newb = S_r.next()
            P.op("dve", lambda e: e.scalar_tensor_tensor(out=S_new[:, :], in0=Sa[:, :], scalar=chn[:, 32 + h:33 + h], in1=KVn[:, :],
                                                         op0=ALU.mult, op1=ALU.add), [Sab, KVnb, B("constB")], [S_newb])
            H[h]["S"] = (S_new, S_newb)
            chain(h, 16 + h, 24 + h, KVo, KVob)
            if i == its[-1] or (CFG["sample"] and i == its[-2]):
                Sf, Sfb = H[h]["S"]
                P.dma("sp", [(ret_p[h, :, :], Sf[:, :])], "rp", reads=[Sfb], final=True)
        P.op("act", lambda e: e.activation(out=orsb[:, :], in_=pOr, func=AF.Copy), [B("pOr")], [orsbb])
        st["orsb"] = (orsb, orsbb)

    def stB2(h, i, st):
        samp = (i == SAMP)
        QT, QTb = st["QT"]
        if samp:
            for s in range(4):
                sample_stream(h, s, QT, QTb)
        else:
            prompt_attention(h, i, QT, QTb)
        oasb, oasbb = oasb_r.next()
        P.op("dve", lambda e: e.tensor_copy(out=oasb[:, :], in_=pOa), [B("pOa")], [oasbb])
        st["oasb"] = (oasb, oasbb)

    def stC(h, i, st):
        wa, wab, wb, wbb, wc, wcb = H[h]["w"]
        lb = i
        tok = slice(lb * 128, (lb + 1) * 128)
        c = cm
        hTt, hTtb = st["hTC"]
        oasb, oasbb = st["oasb"]
        orsb, orsbb = st["orsb"]
        with P.atomic():
            for kt in range(8):
                P.op("pe", lambda e, kt=kt: e.matmul(pC, lhsT=hTt[:, kt, :], rhs=wc[:, kt, :], start=(kt == 0), stop=(kt == 7)),
                     [hTtb, wcb], [B("pC")])
        P.op("act", lambda e: e.activation(out=c["E"][:, :], in_=pC, func=AF.Exp, scale=-1.0), [B("pC")], [B("cm_E")])
        P.op("act", lambda e: e.activation(out=c["gret"][:, :], in_=pC[:, 0:128], func=AF.Copy), [B("pC")], [B("cm_gret")])
        P.op("act", lambda e: e.activation(out=c["L"][:, :], in_=c["E"][:, :], func=AF.Ln, bias=oneT[:, :]), [B("cm_E"), B("oneT")], [B("cm_L")])
        P.op("act", lambda e: e.activation(out=c["Sg"][:, :], in_=c["L"][:, :], func=AF.Exp, scale=-1.0), [B("cm_L")], [B("cm_Sg")])
        P.op("dve", lambda e: e.reciprocal(out=c["rl"][:, :], in_=bass.AP(oasb, 128, [[258, 128], [129, 2]])), [oasbb], [B("cm_rl")])
        P.op("dve", lambda e: e.tensor_tensor(out=c["r2l"][:, :], in0=c["rl"][:, 1:2], in1=neglam[:, :], op=ALU.mult),
             [B("cm_rl"), B("neglam")], [B("cm_r2l")])
        P.op("dve", lambda e: e.tensor_scalar(out=c["o1"][:, :], in0=oasb[:, 0:128], scalar1=c["rl"][:, 0:1], scalar2=None, op0=ALU.mult),
             [oasbb, B("cm_rl")], [B("cm_o1")])
        P.op("dve", lambda e: e.scalar_tensor_tensor(out=c["oa"][:, :], in0=oasb[:, 129:257], scalar=c["r2l"][:, :], in1=c["o1"][:, :],
                                                     op0=ALU.mult, op1=ALU.add), [oasbb, B("cm_r2l"), B("cm_o1")], [B("cm_oa")])
        jk1, jk1b = junk(128)
        jk2, jk2b = junk(128)
        P.op("act", lambda e: e.activation(out=jk1, in_=c["oa"][:, :], func=AF.Square, accum_out=c["ssn"][:, 0:1]),
             [B("cm_oa")], [jk1b, B("cm_ssn0")])
        P.op("act", lambda e: e.activation(out=jk2, in_=orsb[:, :], func=AF.Square, accum_out=c["ssn"][:, 1:2]),
             [orsbb], [jk2b, B("cm_ssn1")])
        P.op("act", lambda e: e.activation(out=c["lnn"][:, :], in_=c["ssn"][:, :], func=AF.Ln, scale=1.0 / 128, bias=epsT[:, :]),
             [B("cm_ssn0"), B("cm_ssn1"), B("eps")], [B("cm_lnn")])
        P.op("act", lambda e: e.activation(out=c["rsn"][:, :], in_=c["lnn"][:, :], func=AF.Exp, scale=-0.5), [B("cm_lnn")], [B("cm_rsn")])
        P.op("dve", lambda e: e.scalar_tensor_tensor(out=c["An"][:, :], in0=c["oa"][:, :], scalar=c["rsn"][:, 0:1], in1=gsub[:, :],
                                                     op0=ALU.mult, op1=ALU.mult), [B("cm_oa"), B("cm_rsn"), B("gsubs")], [B("cm_An")])
        P.op("dve", lambda e: e.scalar_tensor_tensor(out=c["Rn"][:, :], in0=orsb[:, :], scalar=c["rsn"][:, 1:2], in1=grn[:, h * 128:(h + 1) * 128],
                                                     op0=ALU.mult, op1=ALU.mult), [orsbb, B("cm_rsn"), B("constB")], [B("cm_Rn")])
        P.op("pool", lambda e: e.tensor_tensor(out=c["T1"][:, :], in0=c["An"][:, :], in1=c["Sg"][:, 128:256], op=ALU.mult),
             [B("cm_An"), B("cm_Sg")], [B("cm_T1")])
        P.op("pool", lambda e: e.tensor_tensor(out=c["T2"][:, :], in0=c["gret"][:, :], in1=c["Sg"][:, 0:128], op=ALU.mult),
             [B("cm_gret"), B("cm_Sg")], [B("cm_T2")])
        P.op("dve", lambda e: e.tensor_tensor(out=c["T3"][:, :], in0=c["Rn"][:, :], in1=c["T2"][:, :], op=ALU.mult),
             [B("cm_Rn"), B("cm_T2")], [B("cm_T3")])
        P.op("pool", lambda e: e.tensor_tensor(out=c["T4"][:, :], in0=c["T3"][:, :], in1=c["Sg"][:, 256:384], op=ALU.mult),
             [B("cm_T3"), B("cm_Sg")], [B("cm_T4")])
        mixb, mixbb = mixb_r.next()
        P.op("dve", lambda e: e.tensor_tensor(out=mixb[:, :], in0=c["T1"][:, :], in1=c["T4"][:, :], op=ALU.add),
             [B("cm_T1"), B("cm_T4")], [mixbb])
        transpose_to(mixb[:, :], [mixbb], 128, [(mixT[:, h, tok], lambda pt: pt, [B("mixT", lb)])])

    load_weights(heads[0])
    fetch_hTA(*seq[0])
    states = [dict() for _ in seq]
    n = len(seq)
    ent = {}

    def cap(kind, t, fn, *args):
        P.begin()
        fn(*args)
        ent[(kind, t)] = dict(ops=P.end(), pos=0)

    for step in range(n + 2):
        if step < n:
            cap("A", step, stA, seq[step][0], seq[step][1], states[step], seq[step + 1] if step + 1 < n else None)
            cap("A2", step, stA2, seq[step][0], seq[step][1], states[step])
        if 1 <= step <= n:
            t = step - 1
            cap("B", t, stB, seq[t][0], seq[t][1], states[t])
            cap("B2", t, stB2, seq[t][0], seq[t][1], states[t])
        if step >= 2:
            t = step - 2
            cap("C", t, stC, seq[t][0], seq[t][1], states[t])

    def gates(kind, t):
        if kind == "A":
            return [("A", t - 1), ("B", t - 2), ("B2", t - 2)]
        if kind == "A2":
            return [("A2", t - 1), ("A", t - 1), ("B", t - 2), ("B2", t - 2)]
        if kind == "B":
            return [("A", t), ("A2", t), ("B", t - 1), ("C", t - 2)]
        if kind == "B2":
            return [("A", t), ("A2", t), ("B2", t - 1), ("C", t - 2)]
        return [("B", t), ("B2", t), ("C", t - 1)]

    def finished(key):
        e = ent.get(key)
        return e is None or e["pos"] >= len(e["ops"])

    pending = sorted(ent.keys(), key=lambda k: (k[1], k[0]))
    active = []
    while pending or active:
        still = []
        for k in pending:
            if all(finished(g) for g in gates(*k)):
                if ent[k]["ops"]:
                    active.append(k)
            else:
                still.append(k)
        pending = still
        if not active:
            assert not pending, "scheduler gate deadlock"
            break
        ratios = {k: ent[k]["pos"] / len(ent[k]["ops"]) for k in active}
        rmin = min(ratios.values())
        est = {k: P._peek(ent[k]["ops"][ent[k]["pos"]][0]) for k in active}
        tmin = min(est.values())
        age = min(k[1] for k in active)
        bk = min(active, key=lambda k: (est[k] - tmin) + 8000.0 * (ratios[k] - rmin) + 0.0 * (k[1] - age))
        e = ent[bk]
        for (kind_, args, kw) in e["ops"][e["pos"]]:
            if kind_ == "op":
                P.op(*args, **kw)
            else:
                P.dma(*args, **kw)
        e["pos"] += 1
        if e["pos"] >= len(e["ops"]):
            active.remove(bk)
    dbg("mixT", mixT[:, :, :], [128, NH, NOWN * 128], BF16, [B("mixT", lb) for lb in range(NOWN)])


def phase_C(nc, P, B, sC, sb, Rot, bank, dbg, env):
    ident = env["ident"]
    epsT = env["epsT"]
    mixT = env["mixT"]
    gvec = env["gvec"]
    x_own, p_own, y_own = env["x_own"], env["p_own"], env["y_own"]
    psum = env["psum"]
    xt_rot = Rot(sC, "xtc", 2, [128, D], F32)
    hb_rot = Rot(sC, "hbc", 2, [128, D], BF16)
    junk = sb(sC, "junkc", [128, D], BF16)
    groups = [list(range(0, 9)), list(range(9, 17))]
    TMAX = 9 * 128
    fgroups = [list(range(0, 4)), list(range(4, 8)), list(range(8, 12)), list(range(12, 16)), list(range(16, 19)), list(range(19, 22))]

    Wo = sb(sC, "Wo", [128, 8, D], BF16)
    Wpg = Wo
    Wple = sb(sC, "Wple", [128, 2, D], BF16)
    x1 = sb(sC, "x1", [128, 9, D], F32)
    hT2 = sb(sC, "hT2", [128, 8, TMAX], BF16)
    plT = sb(sC, "plT", [128, 2, TMAX], BF16)
    act_r = Rot(sC, "actT", 2, [128, 4, TMAX], BF16)
    Wd_r = Rot(sC, "Wd", 2, [128, 4, D], BF16)
    Wg_r = Rot(sC, "Wg", 2, [128, 8, 256], BF16)
    Wu_r = Rot(sC, "Wu", 2, [128, 8, 256], BF16)
    ssC = sb(sC, "ssC", [128, 9], F32)
    lnC = sb(sC, "lnC", [128, 9], F32)
    rsC = sb(sC, "rsC", [128, 9], F32)
    sg_r = Rot(sC, "sg", 2, [128, 512], BF16)
    pb_r = Rot(sC, "pb", 2, [128, PLE], BF16)
    sgm_r = Rot(sC, "sgm", 2, [128, D], F32)

    def load_sq(src):
        v = src.rearrange("(kt p) n -> p kt n", p=128)
        P.dma("pool", [(Wo[:, 0:4, :], v[:, 0:4, :]), (Wo[:, 4:8, :], v[:, 4:8, :])], "Wo", writes=[B("Wo")])
    P.dma("pool", [(Wple[:, :, :], env["w_ple"].rearrange("(kt p) n -> p kt n", p=128))], "Wple", writes=[B("Wple")])

    pT8 = bank(6).bitcast(BF16)
    w_g = env["w_g"].rearrange("(kt p) f -> p kt f", p=128)
    w_u = env["w_u"].rearrange("(kt p) f -> p kt f", p=128)
    w_d = env["w_d"].rearrange("(f p) n -> p f n", p=128)

    def norm_T(grp, gl, src_ap_fn, src_bufs_fn, ssb):
        for bi, lb in enumerate(grp):
            hb, hbb = hb_rot.next()
            P.op("dve", lambda e, bi=bi, hb=hb: e.scalar_tensor_tensor(out=hb[:, :], in0=x1[:, bi, :], scalar=rsC[:, bi:bi + 1], in1=gvec[:, :],
                                                                       op0=ALU.mult, op1=ALU.mult),
                 [B("x1", bi), ssb, B("gvec")], [hbb])
            for kt in range(8):
                P.op("pe", lambda e, hb=hb, kt=kt: e.transpose(out=pT8[:, kt * 128:(kt + 1) * 128], in_=hb[:, kt * 128:(kt + 1) * 128],
                                                                identity=ident[:, :]), [hbb, B("ident")], [B("bank", 6)])
            P.op("act", lambda e, bi=bi: e.activation(out=hT2[:, :, bi * 128:(bi + 1) * 128], in_=pT8.rearrange("p (k t) -> p k t", k=8),
                                                     func=AF.Copy), [B("bank", 6)], [B("hT2", bi)])

    def stats(grp, tag):
        for bi, lb in enumerate(grp):
            P.op("act", lambda e, bi=bi: e.activation(out=junk[:, :], in_=x1[:, bi, :], func=AF.Square, accum_out=ssC[:, bi:bi + 1]),
                 [B("x1", bi)], [B("junk"), B("ssC")])
        n = len(grp)
        P.op("act", lambda e: e.activation(out=lnC[:, 0:n], in_=ssC[:, 0:n], func=AF.Ln, scale=1.0 / D, bias=epsT[:, :]),
             [B("ssC"), B("eps")], [B("lnC")])
        P.op("act", lambda e: e.activation(out=rsC[:, 0:n], in_=lnC[:, 0:n], func=AF.Exp, scale=-0.5), [B("lnC")], [B("rsC")])

    for gi, grp in enumerate(groups):
        T = len(grp) * 128
        load_sq(env["w_o"])
        P.dma("sp", [(gvec[:, :], bc_rows(env["g_ffn"].tensor, 0, D))], "c1", writes=[B("gvec")])
        for bi, lb in enumerate(grp):
            tok = slice(lb * 128, (lb + 1) * 128)
            xt, xb = xt_rot.next()
            P.dma("sp", [(xt[:, :], x_own[tok, :])], "xtc%d" % (xt_rot.i % 2), writes=[xb])
            pb, pbb = pb_r.next()
            P.dma("pool", [(pb[:, :], p_own[tok, :])], "pb%d" % (pb_r.i % 2), writes=[pbb])
            for k2 in range(2):
                P.op("pe", lambda e, pb=pb, k2=k2: e.transpose(out=pT8[:, k2 * 128:(k2 + 1) * 128], in_=pb[:, k2 * 128:(k2 + 1) * 128],
                                                                identity=ident[:, :]), [pbb, B("ident")], [B("bank", 6)])
            P.op("act", lambda e, bi=bi: e.activation(out=plT[:, :, bi * 128:(bi + 1) * 128],
                                                     in_=pT8[:, 0:256].rearrange("p (k t) -> p k t", k=2), func=AF.Copy),
                 [B("bank", 6)], [B("plT", bi)])
            b0 = 2 * (bi % 2)
            for n in range(2):
                for hh in range(8):
                    P.op("pe", lambda e, n=n, hh=hh, tok=tok, b0=b0: e.matmul(bank(b0 + n), lhsT=mixT[:, hh, tok], rhs=Wo[:, hh, n * 512:(n + 1) * 512],
                                                                               start=(hh == 0), stop=(hh == 7)),
                         [B("mixT", lb), B("Wo")], [B("bank", b0 + n)])
            P.op("dve", lambda e, bi=bi, xt=xt, b0=b0: e.tensor_tensor(out=x1[:, bi, :], in0=psum[:, b0 * 512:(b0 + 2) * 512], in1=xt[:, :], op=ALU.add),
                 [B("bank", b0), B("bank", b0 + 1), xb], [B("x1", bi)])
        stats(grp, "ffn")
        norm_T(grp, gi, None, None, B("rsC"))
        for fg in fgroups:
            nf = len(fg)
            actT, actb = act_r.next()
            Wd, Wdb = Wd_r.next()
            P.dma("pool", [(Wd[:, 0:nf, :], w_d[:, fg[0]:fg[0] + nf, :])], "Wd%d" % (Wd_r.i % 2), writes=[Wdb])
            for f2 in range(0, nf, 2):
                f0 = fg[0] + f2
                n2 = min(2, nf - f2)
                Wg, Wgb = Wg_r.next()
                Wu, Wub = Wu_r.next()
                P.dma("pool", [(Wg[:, :, 0:n2 * 128], w_g[:, :, f0 * 128:(f0 + n2) * 128])], "Wg%d" % (Wg_r.i % 2), writes=[Wgb])
                P.dma("pool", [(Wu[:, :, 0:n2 * 128], w_u[:, :, f0 * 128:(f0 + n2) * 128])], "Wu%d" % (Wu_r.i % 2), writes=[Wub])
                for fi in range(n2):
                    fl = f2 + fi
                    for c0 in range(0, T, 512):
                        cw = min(512, T - c0)
                        pg = bank(2 + (c0 // 512) % 2, 0, cw)
                        pu = bank(4 + (c0 // 512) % 2, 0, cw)
                        gb_ = B("bank", 2 + (c0 // 512) % 2)
                        ub_ = B("bank", 4 + (c0 // 512) % 2)
                        tb = [B("hT2", bi) for bi in range(c0 // 128, (c0 + cw) // 128)]
                        for kt in range(8):
                            P.op("pe", lambda e, kt=kt, fi=fi, Wg=Wg, pg=pg, c0=c0, cw=cw: e.matmul(
                                pg, lhsT=Wg[:, kt, fi * 128:(fi + 1) * 128], rhs=hT2[:, kt, c0:c0 + cw], start=(kt == 0), stop=(kt == 7)),
                                tb + [Wgb], [gb_])
                        for kt in range(8):
                            P.op("pe", lambda e, kt=kt, fi=fi, Wu=Wu, pu=pu, c0=c0, cw=cw: e.matmul(
                                pu, lhsT=Wu[:, kt, fi * 128:(fi + 1) * 128], rhs=hT2[:, kt, c0:c0 + cw], start=(kt == 0), stop=(kt == 7)),
                                tb + [Wub], [ub_])
                        sg, sgb = sg_r.next()
                        P.op("act", lambda e, sg=sg, pg=pg, cw=cw: e.activation(out=sg[:, 0:cw], in_=pg, func=AF.Silu), [gb_], [sgb])
                        P.op("dve", lambda e, sg=sg, pu=pu, cw=cw, c0=c0, fl=fl, actT=actT: e.tensor_tensor(
                            out=actT[:, fl, c0:c0 + cw], in0=pu, in1=sg[:, 0:cw], op=ALU.mult), [ub_, sgb], [actb])
            for bi, lb in enumerate(grp):
                b0 = 6 * (bi % 2)
                for n in range(2):
                    for fl in range(nf):
                        P.op("pe", lambda e, n=n, fl=fl, bi=bi, actT=actT, Wd=Wd, nf=nf, b0=b0: e.matmul(
                            bank(b0 + n), lhsT=actT[:, fl, bi * 128:(bi + 1) * 128], rhs=Wd[:, fl, n * 512:(n + 1) * 512],
                            start=(fl == 0), stop=(fl == nf - 1)), [actb, Wdb], [B("bank", b0 + n)])
                P.op("dve", lambda e, bi=bi, b0=b0: e.tensor_tensor(out=x1[:, bi, :], in0=psum[:, b0 * 512:(b0 + 2) * 512], in1=x1[:, bi, :], op=ALU.add),
                     [B("bank", b0), B("bank", b0 + 1), B("x1", bi)], [B("x1", bi)])
        load_sq(env["w_pg"])
        P.dma("sp", [(gvec[:, :], bc_rows(env["g_ple"].tensor, 0, D))], "c1", writes=[B("gvec")])
        stats(grp, "ple")
        norm_T(grp, gi, None, None, B("rsC"))
        for bi, lb in enumerate(grp):
            tok = slice(lb * 128, (lb + 1) * 128)
            bz = 4 * (bi % 2)
            bp = 2 + 4 * (bi % 2)
            for n in range(2):
                for kt in range(8):
                    P.op("pe", lambda e, n=n, kt=kt, bi=bi, bz=bz: e.matmul(bank(bz + n), lhsT=hT2[:, kt, bi * 128:(bi + 1) * 128],
                                                                             rhs=Wpg[:, kt, n * 512:(n + 1) * 512], start=(kt == 0), stop=(kt == 7)),
                         [B("hT2", bi), B("Wo")], [B("bank", bz + n)])
            for n in range(2):
                for k2 in range(2):
                    P.op("pe", lambda e, n=n, k2=k2, bi=bi, bp=bp: e.matmul(bank(bp + n), lhsT=plT[:, k2, bi * 128:(bi + 1) * 128],
                                                                             rhs=Wple[:, k2, n * 512:(n + 1) * 512], start=(k2 == 0), stop=(k2 == 1)),
                         [B("plT", bi), B("Wple")], [B("bank", bp + n)])
            sgm, sgmb = sgm_r.next()
            P.op("act", lambda e, sgm=sgm, bz=bz: e.activation(out=sgm[:, :], in_=psum[:, bz * 512:(bz + 2) * 512], func=AF.Sigmoid),
                 [B("bank", bz), B("bank", bz + 1)], [sgmb])
            P.op("dve", lambda e, sgm=sgm, bp=bp: e.tensor_tensor(out=sgm[:, :], in0=psum[:, bp * 512:(bp + 2) * 512], in1=sgm[:, :], op=ALU.mult),
                 [B("bank", bp), B("bank", bp + 1), sgmb], [sgmb])
            P.op("pool", lambda e, sgm=sgm, bi=bi: e.tensor_tensor(out=sgm[:, :], in0=sgm[:, :], in1=x1[:, bi, :], op=ALU.add),
                 [sgmb, B("x1", bi)], [sgmb])
            P.dma("sp", [(y_own[tok, :], sgm[:, :])], "yo%d" % (sgm_r.i % 2), reads=[sgmb], final=True)


_PROG_CACHE = {}


def _consts(p):
    f32 = np.float32
    c = {}
    c["c_ident"] = np.eye(128, dtype=f32)
    r = np.arange(128)
    pos = np.zeros((NLB, 128), dtype=np.int64)
    for i in range(16):
        pos[i] = (2 * i + p) * 128 + r
        pos[NOWN + i] = (2 * i + 1 - p) * 128 + r
    pos[SAMP] = PAST + (r % 32)
    inv_freq = (np.float32(10000.0) ** (-np.arange(32, dtype=f32) / np.float32(32))).astype(f32)
    ang = pos.astype(f32)[:, :, None] * inv_freq[None, None, :]
    cos = np.cos(ang).astype(f32)
    sin = np.sin(ang).astype(f32)
    c["c_cos"] = np.ascontiguousarray(cos.transpose(1, 0, 2))
    c["c_sin"] = np.ascontiguousarray(np.concatenate([-sin, sin], axis=-1).transpose(1, 0, 2))
    log_g = np.log1p(-np.exp2(-5.0 - np.arange(NH, dtype=np.float64)))
    kscale = 64 ** -0.5
    j = r.astype(np.float64)
    c["c_kdc"] = (np.exp(log_g[None, :] * (127.0 - j[:, None])) * kscale).astype(f32)
    kd = np.zeros((128, NH, 4))
    for s in range(4):
        m = (r // 32 == s)
        kd[m, :, s] = np.exp(log_g[None, :] * (31.0 - (j[m] % 32)[:, None])) * kscale
    c["c_kdcs"] = kd.astype(f32)
    c["c_qdc"] = np.exp(log_g[None, :] * (j[:, None] + 1.0)).astype(f32)
    c["c_qdcs"] = np.exp(log_g[None, :] * ((j % 32)[:, None] + 1.0)).astype(f32)
    diff = j[None, :] - j[:, None]
    dt = np.where(diff[:, None, :] >= 0, np.exp(log_g[None, :, None] * np.maximum(diff[:, None, :], 0.0)), 0.0) * kscale
    c["c_dt"] = dt.astype(f32)
    same = (r[:, None] // 32 == r[None, :] // 32)
    c["c_dts"] = (dt * same[:, None, :]).astype(f32)
    chn = np.zeros((64, 48))
    g128 = np.exp(log_g * 128.0)
    g32 = np.exp(log_g * 32.0)
    chn[:, 0:8] = g128 if p == 1 else 1.0
    chn[:, 8:16] = 1.0 if p == 1 else 0.0
    chn[:, 16:24] = g128 if p == 0 else 1.0
    chn[:, 24:32] = 1.0 if p == 0 else 0.0
    chn[:, 32:40] = g128
    chn[:, 40:48] = g32
    c["c_chn"] = chn.astype(f32)
    c["c_obias"] = np.full((128, 1), 0.0 if p == 1 else -30000.0, dtype=f32)
    c["c_sbias"] = np.where(r[:, None] // 32 == np.arange(4)[None, :], 0.0, -30000.0).astype(f32)
    return c


def _perm_w_in(w_in):
    w = w_in
    segs = []
    for h in range(NH):
        cols = [w[:, 1024 + h * 128: 1024 + (h + 1) * 128],
                w[:, 3584 + h * 64: 3584 + (h + 1) * 64],
                w[:, 2048 + h * 128: 2048 + (h + 1) * 128],
                w[:, 4096 + h * 128: 4096 + (h + 1) * 128],
                w[:, 0 + h * 128: (h + 1) * 128],
                w[:, 3072 + h * 64: 3072 + (h + 1) * 64],
                w[:, 5120 + h * 128: 5120 + (h + 1) * 128],
                w[:, 6144 + h * 128: 6144 + (h + 1) * 128],
                w[:, 7168 + h * 128: 7168 + (h + 1) * 128]]
        segs.append(np.concatenate(cols, axis=1))
    return np.ascontiguousarray(np.stack(segs, axis=1))


def make_in_maps(inputs):
    f = lambda a: np.ascontiguousarray(np.asarray(a, dtype=np.float32))
    xp = f(inputs["x_prompt"]).reshape(4, 32, 128, D)
    xs = f(inputs["x_sample"])
    pp = f(inputs["p_prompt"])[0].reshape(4, 32, 128, PLE)
    psm = f(inputs["p_sample"])[0]
    ckf = f(inputs["cache_attn_k"])[0].reshape(32, PAST, D)
    cvf = f(inputs["cache_attn_v"])[0].reshape(32, PAST, D)
    srf = f(inputs["state_ret"])[0]
    shared = {
        "w_in": _perm_w_in(f(inputs["w_in"])[0]),
        "w_o": f(inputs["w_o"])[0], "w_g": f(inputs["w_ff_gate"])[0], "w_u": f(inputs["w_ff_up"])[0],
        "w_d": f(inputs["w_ff_down"])[0], "w_ple": f(inputs["w_ple"])[0], "w_pg": f(inputs["w_ple_gate"])[0],
        "g_mix": f(inputs["g_mix_norm"]).reshape(1, D), "g_ffn": f(inputs["g_ffn_norm"]).reshape(1, D),
        "g_ple": f(inputs["g_ple_norm"]).reshape(1, D), "g_q": f(inputs["g_q_norm"]).reshape(1, 64),
        "g_k": f(inputs["g_k_norm"]).reshape(1, 64), "g_sub": f(inputs["g_sub_norm"]).reshape(1, 128),
        "g_rn": f(inputs["g_ret_norm"]).reshape(1, NH * 128), "lam_q": f(inputs["lam_q"]).reshape(1, 128),
        "lam_k": f(inputs["lam_k"]).reshape(1, 128),
    }
    consts = [_consts(0), _consts(1)]
    maps = []
    for c in range(8):
        b, p = c // 2, c % 2
        own = list(range(p, 32, 2))
        oth = list(range(1 - p, 32, 2))
        m = dict(shared)
        m.update(consts[p])
        m["x_own"] = np.ascontiguousarray(np.concatenate([xp[b, own].reshape(2048, D), xs[4 * c:4 * c + 4].reshape(128, D)], axis=0))
        m["x_oth"] = np.ascontiguousarray(xp[b, oth].reshape(2048, D))
        m["p_own"] = np.ascontiguousarray(np.concatenate([pp[b, own].reshape(2048, PLE), psm[4 * c:4 * c + 4].reshape(128, PLE)], axis=0))
        m["ck"] = np.ascontiguousarray(ckf[4 * c:4 * c + 4])
        m["cv"] = np.ascontiguousarray(cvf[4 * c:4 * c + 4])
        m["sr"] = np.ascontiguousarray(srf[4 * c:4 * c + 4])
        maps.append(m)
    return maps


def assemble(results):
    y_p = np.zeros((4, 32, 128, D), np.float32)
    y_s = np.zeros((32, 32, D), np.float32)
    k_p = np.zeros((4, 32, 128, D), np.float32)
    v_p = np.zeros((4, 32, 128, D), np.float32)
    r_p = np.zeros((4, NH, 64, 128), np.float32)
    k_s = np.zeros((32, 32, D), np.float32)
    v_s = np.zeros((32, 32, D), np.float32)
    r_s = np.zeros((32, NH, 64, 128), np.float32)
    for c in range(8):
        b, p = c // 2, c % 2
        own = list(range(p, 32, 2))
        r = results[c]
        y_p[b, own] = r["y_own"][:2048].reshape(16, 128, D)
        k_p[b, own] = r["k_own"][:2048].reshape(16, 128, D)
        v_p[b, own] = r["v_own"][:2048].reshape(16, 128, D)
        y_s[4 * c:4 * c + 4] = r["y_own"][2048:].reshape(4, 32, D)
        k_s[4 * c:4 * c + 4] = r["k_own"][2048:].reshape(4, 32, D)
        v_s[4 * c:4 * c + 4] = r["v_own"][2048:].reshape(4, 32, D)
        r_s[4 * c:4 * c + 4] = r["ret_s"]
        if p == 1:
            r_p[b] = r["ret_p"]
    return (y_p.reshape(4, SEQ, D), y_s, k_p.reshape(1, 4, SEQ, NH, 128), v_p.reshape(1, 4, SEQ, NH, 128),
            r_p.reshape(1, 4, NH, 64, 128), k_s.reshape(1, 32, 32, NH, 128), v_s.reshape(1, 32, 32, NH, 128),
            r_s.reshape(1, 32, NH, 64, 128))


def kernel(**inputs):
    if "nc" not in _PROG_CACHE:
        _PROG_CACHE["nc"] = build_program()[0]
    nc = _PROG_CACHE["nc"]
    maps = make_in_maps(inputs)
    res = run_bass_kernel_spmd(nc, maps, core_ids=list(range(8)))
    return assemble(res.results)
```

```python
import math
import numpy as np
from contextlib import ExitStack
import concourse.bass as bass
import concourse.mybir as mybir
from concourse.bass_utils import run_bass_kernel_spmd

F32 = mybir.dt.float32
BF16 = mybir.dt.bfloat16
AF = mybir.ActivationFunctionType
ALU = mybir.AluOpType
AX = mybir.AxisListType

COMPUTE = ("pe", "act", "dve", "pool")
ENGS = ("pe", "act", "dve", "pool", "sp")

D = 1024
NH = 8
SEQ = 4096
PAST = 2048
DFF = 2816
NF = DFF // 128
PLE = 256
EPS = 1e-6
NOWN = 17
NOTH = 16
NLB = NOWN + NOTH
SAMP = 16
LAM_INIT = 0.8 - 0.6 * math.exp(-0.3 * 0)
CFG = dict(nheads=NH, niters=16, sample=True)
SAME_ENGINE_ALL = True
SCHED = True
DUR = {"pe": 100.0, "act": 380.0, "dve": 280.0, "pool": 450.0, "sp": 60.0}
SEM_LAT = 250.0
OK_, OKR, OV, OVR, OQ, OQR, OGRET, OGA, OGR = 0, 128, 192, 320, 448, 576, 640, 768, 896


class Buf:
    __slots__ = ("name", "w", "r")

    def __init__(self, name):
        self.name = name
        self.w = None
        self.r = []


class Prog:
    def __init__(self, nc):
        self.nc = nc
        self.streams = {e: [] for e in ENGS}
        self.marked = {e: set() for e in COMPUTE}
        self.dma_cum = {}
        self.bufs = {}
        self.final_tokens = []
        self.bank_last = {}
        self.bankmap = lambda name: ()
        self.cap = None
        self._atomic = 0
        self.eng_free = {e: 0.0 for e in ENGS}
        self.done = {}
        self.sched = SCHED

    def buf(self, *key):
        b = self.bufs.get(key)
        if b is None:
            b = Buf(key)
            self.bufs[key] = b
        return b

    def _deps(self, reads, writes, tok, eng=None):
        raw = set()
        oth = set()
        for b in reads:
            if b.w is not None:
                raw.add(b.w)
        for b in writes:
            if b.w is not None:
                oth.add(b.w)
            for t in b.r:
                oth.add(t)
        for b in reads:
            b.r.append(tok)
        for b in writes:
            b.w = tok
            b.r = []
        waits = set()
        for w in raw | oth:
            if w == tok:
                continue
            if eng is not None and w[0] == "c" and w[1] == eng:
                if eng == "pe" or (w not in raw and not SAME_ENGINE_ALL):
                    continue
            waits.add(w)
        return waits

    def _mark(self, waits):
        for w in waits:
            if w[0] == "c":
                self.marked[w[1]].add(w[2])

    def begin(self):
        self.cap = []
        self._atomic = 0

    def end(self):
        c, self.cap = self.cap, None
        return c

    def atomic(self):
        prog = self

        class _A:
            def __enter__(self_):
                if prog.cap is not None:
                    if prog._atomic == 0:
                        prog.cap.append([])
                    prog._atomic += 1

            def __exit__(self_, *a):
                if prog.cap is not None:
                    prog._atomic -= 1
                return False
        return _A()

    def _record(self, item):
        if self._atomic:
            self.cap[-1].append(item)
        else:
            self.cap.append([item])

    def _peek(self, item):
        kind, args, kw = item
        if kind == "op":
            eng, fn, reads, writes = args[0], args[1], args[2], args[3]
        else:
            eng, reads, writes = args[0], kw.get("reads", ()), kw.get("writes", ())
        t = self.eng_free[eng]
        toks = []
        for b in reads:
            if b.w is not None:
                toks.append(b.w)
        for b in writes:
            if b.w is not None:
                toks.append(b.w)
            toks.extend(b.r)
        if kind == "op":
            banks = set()
            for b in list(reads) + list(writes):
                banks |= set(self.bankmap(b.name))
            for bk in banks:
                for e2, t2 in self.bank_last.get(bk, {}).items():
                    if e2 != eng:
                        toks.append(t2)
        for tk in toks:
            d = self.done.get(tk, 0.0)
            if not (tk[0] == "c" and tk[1] == eng):
                d += SEM_LAT
            if d > t:
                t = d
        return t

    def interleave(self, lists):
        lists = [l for l in lists if l]
        pos = [0] * len(lists)
        while True:
            cand = [k for k, l in enumerate(lists) if pos[k] < len(l)]
            if not cand:
                break
            ratios = {k: pos[k] / len(lists[k]) for k in cand}
            if self.sched:
                rmin = min(ratios.values())
                tmin = None
                est = {}
                for k in cand:
                    est[k] = self._peek(lists[k][pos[k]][0])
                    tmin = est[k] if tmin is None else min(tmin, est[k])
                bk = min(cand, key=lambda k: (est[k] - tmin) + 4000.0 * (ratios[k] - rmin))
            else:
                bk = min(cand, key=lambda k: ratios[k])
            for (kind, args, kw) in lists[bk][pos[bk]]:
                if kind == "op":
                    self.op(*args, **kw)
                else:
                    self.dma(*args, **kw)
            pos[bk] += 1

    def op(self, eng, fn, reads=(), writes=(), cost=None):
        if self.cap is not None:
            self._record(("op", (eng, fn, tuple(reads), tuple(writes)), dict(cost=cost)))
            return None
        idx = len(self.streams[eng])
        tok = ("c", eng, idx)
        waits = self._deps(reads, writes, tok, eng)
        banks = set()
        for b in list(reads) + list(writes):
            banks |= set(self.bankmap(b.name))
        for bk in banks:
            last = self.bank_last.setdefault(bk, {})
            for e2, t2 in last.items():
                if e2 != eng:
                    waits.add(t2)
            last[eng] = tok
        self._mark(waits)
        self.streams[eng].append(dict(kind="c", fn=fn, waits=waits))
        t = self.eng_free[eng]
        for w in waits:
            d = self.done.get(w, 0.0) + (0.0 if (w[0] == "c" and w[1] == eng) else SEM_LAT)
            if d > t:
                t = d
        t += (cost if cost is not None else DUR[eng])
        self.eng_free[eng] = t
        self.done[tok] = t
        return tok

    def dma(self, queue, pairs, key, reads=(), writes=(), final=False):
        if self.cap is not None:
            self._record(("dma", (queue, list(pairs), key), dict(reads=tuple(reads), writes=tuple(writes), final=final)))
            return None
        base = self.dma_cum.get(key, 0)
        cum = base + 16 * len(pairs)
        self.dma_cum[key] = cum
        tok = ("d", key, cum)
        waits = self._deps(reads, writes, tok)
        self._mark(waits)
        first = True
        for (o, i) in pairs:
            self.streams[queue].append(dict(kind="d", out=o, in_=i, key=key, waits=waits if first else set()))
            first = False
        if final:
            self.final_tokens.append(tok)
        t = self.eng_free[queue]
        for w in waits:
            d = self.done.get(w, 0.0) + SEM_LAT
            if d > t:
                t = d
        self.eng_free[queue] = t + DUR["sp"] * len(pairs)
        self.done[tok] = t + 2500.0
        return tok

    def barrier(self):
        toks = set()
        for e in COMPUTE:
            for idx in range(len(self.streams[e]) - 1, -1, -1):
                if self.streams[e][idx]["kind"] == "c":
                    toks.add(("c", e, idx))
                    break
        for k, cum in self.dma_cum.items():
            toks.add(("d", k, cum))
        self._mark(toks)
        for e in ENGS:
            self.streams[e].append(dict(kind="w", waits={t for t in toks if not (t[0] == "c" and t[1] == e)}))

    def emit(self, stack):
        nc = self.nc
        sems = {}
        for e in COMPUTE:
            sems[("c", e)] = stack.enter_context(nc.semaphore("s_" + e))
        for k in self.dma_cum:
            sems[("d", k)] = stack.enter_context(nc.semaphore("d_" + str(k)))
        tick = {}
        for e in COMPUTE:
            n = 0
            for idx in range(len(self.streams[e])):
                if self.streams[e][idx]["kind"] == "c" and idx in self.marked[e]:
                    n += 1
                    tick[(e, idx)] = n
        self.streams["sp"].append(dict(kind="w", waits=set(self.final_tokens)))
        block = stack.enter_context(nc.Block())
        handles = {"pe": block.tensor, "act": block.scalar, "dve": block.vector,
                   "pool": block.gpsimd, "sp": block.sync}
        stats = {}
        for e in ENGS:
            stream = self.streams[e]
            if not stream:
                continue
            nwait = [0]

            def body(eng, e=e, stream=stream, nwait=nwait):
                waited = {}
                for idx, o in enumerate(stream):
                    for w in sorted(o["waits"], key=str):
                        if w[0] == "c":
                            sk = ("c", w[1])
                            val = tick[(w[1], w[2])]
                        else:
                            sk = ("d", w[1])
                            val = w[2]
                        if waited.get(sk, 0) >= val:
                            continue
                        waited[sk] = val
                        eng.wait_ge(sems[sk], val)
                        nwait[0] += 1
                    if o["kind"] == "c":
                        ins = o["fn"](eng)
                        if idx in self.marked[e]:
                            ins.then_inc(sems[("c", e)], 1)
                    elif o["kind"] == "d":
                        eng.dma_start(out=o["out"], in_=o["in_"]).then_inc(sems[("d", o["key"])], 16)

            handles[e](body)
            stats[e] = (len(stream), nwait[0])
        return stats


def bc_rows(dram_ap_1d_tensor, offset, n, nparts=128):
    return bass.AP(dram_ap_1d_tensor, offset, [[0, nparts], [1, n]])


def build_program(debug=None, phases="ABC"):
    nc = bass.Bass("TRN2", target_bir_lowering=False)
    dbg_outs = {}

    def din(name, shape, dt=F32):
        return nc.dram_tensor(name, list(shape), dt, kind="ExternalInput").ap()

    def dout(name, shape, dt=F32):
        return nc.dram_tensor(name, list(shape), dt, kind="ExternalOutput").ap()

    x_own = din("x_own", [NOWN * 128, D])
    x_oth = din("x_oth", [NOTH * 128, D])
    p_own = din("p_own", [NOWN * 128, PLE])
    ck = din("ck", [4, PAST, D])
    cv = din("cv", [4, PAST, D])
    sr = din("sr", [4, NH, 64, 128])
    w_in = din("w_in", [D, NH, 1024])
    w_o = din("w_o", [D, D])
    w_g = din("w_g", [D, DFF])
    w_u = din("w_u", [D, DFF])
    w_d = din("w_d", [DFF, D])
    w_ple = din("w_ple", [PLE, D])
    w_pg = din("w_pg", [D, D])
    g_mix = din("g_mix", [1, D])
    g_ffn = din("g_ffn", [1, D])
    g_ple = din("g_ple", [1, D])
    g_q = din("g_q", [1, 64])
    g_k = din("g_k", [1, 64])
    g_sub = din("g_sub", [1, 128])
    g_rn = din("g_rn", [1, NH * 128])
    lam_q = din("lam_q", [1, 128])
    lam_k = din("lam_k", [1, 128])
    c_ident = din("c_ident", [128, 128])
    c_cos = din("c_cos", [128, NLB, 32])
    c_sin = din("c_sin", [128, NLB, 64])
    c_kdc = din("c_kdc", [128, NH])
    c_kdcs = din("c_kdcs", [128, NH, 4])
    c_qdc = din("c_qdc", [128, NH])
    c_qdcs = din("c_qdcs", [128, NH])
    c_dt = din("c_dt", [128, NH, 128])
    c_dts = din("c_dts", [128, NH, 128])
    c_chn = din("c_chn", [64, 48])
    c_obias = din("c_obias", [128, 1])
    c_sbias = din("c_sbias", [128, 4])

    y_own = dout("y_own", [NOWN * 128, D])
    k_own = dout("k_own", [NOWN * 128, D])
    v_own = dout("v_own", [NOWN * 128, D])
    ret_p = dout("ret_p", [NH, 64, 128])
    ret_s = dout("ret_s", [4, NH, 64, 128])

    with ExitStack() as top:
        P = Prog(nc)
        B = P.buf
        mem = {"tot": 0}

        def bankmap(name):
            k = name[0]
            if k == "pA":
                return (0,) if name[1] == 0 else (7,)
            if k in ("pB", "pKV", "pAT"):
                return (1,)
            if k in ("pC", "pOr"):
                return (2,)
            if k == "pOa":
                return (3,)
            if k == "pT":
                return (4,)
            if k == "pS":
                return (5 + name[1],)
            if k == "bank":
                return (name[1],)
            return ()
        P.bankmap = bankmap

        def sb(stack, name, shape, dt):
            n = 1
            for s in shape[1:]:
                n *= s
            mem["tot"] += n * (4 if dt == F32 else 2)
            mem.setdefault("log", []).append((name, n * (4 if dt == F32 else 2)))
            return stack.enter_context(nc.sbuf_tensor(name, list(shape), dt))

        psum = top.enter_context(nc.psum_tensor("psum", [128, 4096], F32))

        def bank(b, c0=0, c1=512):
            return psum[:, b * 512 + c0: b * 512 + c1]

        class Rot:
            def __init__(self, stack, name, n, shape, dt):
                self.tiles = [sb(stack, "%s%d" % (name, i), shape, dt) for i in range(n)]
                self.name = name
                self.n = n
                self.i = -1

            def next(self):
                self.i += 1
                s = self.i % self.n
                return self.tiles[s], B(self.name, s)

        def dbg(name, ap, shape, dt, reads):
            if debug is None or name not in debug:
                return
            o = dout("dbg_" + name, shape, dt)
            dbg_outs[name] = (shape, dt)
            P.dma("sp", [(o, ap)], "dbg_" + name, reads=reads, final=True)

        ident_f = sb(top, "ident_f", [128, 128], F32)
        ident = sb(top, "ident", [128, 128], BF16)
        epsT = sb(top, "epsT", [128, 1], F32)
        mixT = sb(top, "mixT", [128, NH, NOWN * 128], BF16)
        gvec = sb(top, "gvec", [128, D], F32)

        P.dma("sp", [(ident_f[:, :], c_ident)], "c0", writes=[B("ident_f")])
        P.op("dve", lambda e: e.tensor_copy(out=ident[:, :], in_=ident_f[:, :]), [B("ident_f")], [B("ident")])
        P.op("dve", lambda e: e.memset(epsT[:, :], EPS), [], [B("eps")])
        P.dma("sp", [(gvec[:, :], bc_rows(g_mix.tensor, 0, D))], "c1", writes=[B("gvec")])

        hT_scr = nc.dram_tensor("hT_scr", [NLB, 128, 8 * 128], BF16).ap()

        with ExitStack() as sA:
            xt_rot = Rot(sA, "xt", 3, [128, D], F32)
            hb_rot = Rot(sA, "hb", 2, [128, D], BF16)
            hs_rot = Rot(sA, "hs", 2, [128, D], BF16)
            junk = sb(sA, "junk", [128, D], BF16)

            def xblk(lb):
                return x_own[lb * 128:(lb + 1) * 128, :] if lb < NOWN else x_oth[(lb - NOWN) * 128:(lb - NOWN + 1) * 128, :]

            st_r = Rot(sA, "st1", 3, [128, 4], F32)
            order = []
            for i in range(16):
                order += [NOWN + i, i]
            order.append(SAMP)
            for n_, lb in enumerate(order):
                xt, xb = xt_rot.next()
                hb, hbb = hb_rot.next()
                hs, hsb = hs_rot.next()
                st1, st1b = st_r.next()
                P.dma("sp", [(xt[:, :], xblk(lb))], "xt%d" % (xt_rot.i % 3), writes=[xb])
                P.op("act", lambda e, xt=xt, st1=st1: e.activation(out=junk[:, :], in_=xt[:, :], func=AF.Square, accum_out=st1[:, 0:1]),
                     [xb], [B("junk"), st1b])
                P.op("act", lambda e, st1=st1: e.activation(out=st1[:, 1:2], in_=st1[:, 0:1], func=AF.Ln, scale=1.0 / D, bias=epsT[:, :]),
                     [st1b, B("eps")], [st1b])
                P.op("act", lambda e, st1=st1: e.activation(out=st1[:, 2:3], in_=st1[:, 1:2], func=AF.Exp, scale=-0.5), [st1b], [st1b])
                P.op("dve", lambda e, xt=xt, hb=hb, st1=st1: e.scalar_tensor_tensor(
                    out=hb[:, :], in0=xt[:, :], scalar=st1[:, 2:3], in1=gvec[:, :], op0=ALU.mult, op1=ALU.mult),
                    [xb, st1b, B("gvec")], [hbb])
                bk = 4 + (n_ % 2)
                pTa = bank(bk).bitcast(BF16)
                for kt in range(8):
                    P.op("pe", lambda e, hb=hb, kt=kt, pTa=pTa: e.transpose(out=pTa[:, kt * 128:(kt + 1) * 128],
                                                                             in_=hb[:, kt * 128:(kt + 1) * 128], identity=ident[:, :]),
                         [hbb, B("ident")], [B("bank", bk)])
                if n_ % 2 == 0:
                    P.op("act", lambda e, hs=hs, pTa=pTa: e.activation(out=hs[:, :], in_=pTa, func=AF.Copy), [B("bank", bk)], [hsb])
                else:
                    P.op("dve", lambda e, hs=hs, pTa=pTa: e.tensor_copy(out=hs[:, :], in_=pTa), [B("bank", bk)], [hsb])
                P.dma("pool", [(hT_scr[lb], hs[:, :])], "hs%d" % (hs_rot.i % 2), reads=[hsb], writes=[B("hTs", lb)])
        P.barrier()
        if "B" in phases:
            with ExitStack() as sB:
                phase_B(nc, P, B, sB, sb, Rot, bank, dbg, locals())
        if "C" in phases:
            P.barrier()
            with ExitStack() as sC:
                phase_C(nc, P, B, sC, sb, Rot, bank, dbg, locals())
        stats = P.emit(top)
        print("emit stats", stats, "sbuf bytes/partition (sum of all tiles)", mem["tot"], flush=True)
    return nc, dbg_outs


def phase_B(nc, P, B, sB, sb, Rot, bank, dbg, env):
    hT_scr = env["hT_scr"]
    psum = env["psum"]
    ident = env["ident"]
    epsT = env["epsT"]
    mixT = env["mixT"]
    w_in = env["w_in"]
    k_own, v_own, ret_p, ret_s = env["k_own"], env["v_own"], env["ret_p"], env["ret_s"]
    ck, cv, sr = env["ck"], env["cv"], env["sr"]

    cosT = sb(sB, "cosT", [128, NLB, 32], F32)
    sinT = sb(sB, "sinT", [128, NLB, 64], F32)
    kdc = sb(sB, "kdc", [128, NH], F32)
    kdcs = sb(sB, "kdcs", [128, NH, 4], F32)
    qdc = sb(sB, "qdc", [128, NH], F32)
    qdcs = sb(sB, "qdcs", [128, NH], F32)
    DT = sb(sB, "DT", [128, NH, 128], F32)
    DTs = sb(sB, "DTs", [128, NH, 128], F32)
    chn = sb(sB, "chn", [64, 48], F32)
    obias = sb(sB, "obias", [128, 1], F32)
    sbias = sb(sB, "sbias", [128, 4], F32)
    oneT = sb(sB, "oneT", [128, 1], F32)
    gq = sb(sB, "gq", [128, 64], F32)
    gk = sb(sB, "gk", [128, 64], F32)
    gsub = sb(sB, "gsub", [128, 128], F32)
    grn = sb(sB, "grn", [128, NH * 128], F32)
    lq = sb(sB, "lq", [128, 128], F32)
    lk = sb(sB, "lk", [128, 128], F32)
    lprod = sb(sB, "lprod", [128, 128], F32)
    lred = sb(sB, "lred", [128, 2], F32)
    lex = sb(sB, "lex", [128, 2], F32)
    neglam = sb(sB, "neglam", [128, 1], F32)
    CB = [B("constB")]
    P.dma("sp", [(cosT[:, :, :], env["c_cos"]), (sinT[:, :, :], env["c_sin"]), (kdc[:, :], env["c_kdc"]),
                 (kdcs[:, :, :], env["c_kdcs"]), (qdc[:, :], env["c_qdc"]), (qdcs[:, :], env["c_qdcs"]),
                 (DT[:, :, :], env["c_dt"]), (DTs[:, :, :], env["c_dts"]), (chn[:, :], env["c_chn"]),
                 (obias[:, :], env["c_obias"]), (sbias[:, :], env["c_sbias"]),
                 (gq[:, :], bc_rows(env["g_q"].tensor, 0, 64)), (gk[:, :], bc_rows(env["g_k"].tensor, 0, 64)),
                 (gsub[:, :], bc_rows(env["g_sub"].tensor, 0, 128)), (grn[:, :], bc_rows(env["g_rn"].tensor, 0, NH * 128)),
                 (lq[:, :], bc_rows(env["lam_q"].tensor, 0, 128)), (lk[:, :], bc_rows(env["lam_k"].tensor, 0, 128))],
          "c2", writes=CB)
    P.op("dve", lambda e: e.memset(oneT[:, :], 1.0), [], [B("oneT")])
    P.op("dve", lambda e: e.tensor_tensor(out=lprod[:, :], in0=lq[:, :], in1=lk[:, :], op=ALU.mult), CB, [B("lprod")])
    P.op("dve", lambda e: e.tensor_reduce(out=lred[:, :], in_=lprod[:, :].rearrange("p (c d) -> p c d", c=2), axis=AX.X, op=ALU.add),
         [B("lprod")], [B("lred")])
    P.op("act", lambda e: e.activation(out=lex[:, :], in_=lred[:, :], func=AF.Exp), [B("lred")], [B("lex")])
    P.op("dve", lambda e: e.tensor_tensor(out=neglam[:, :], in0=lex[:, 1:2], in1=lex[:, 0:1], op=ALU.subtract), [B("lex")], [B("neglam")])
    P.op("dve", lambda e: e.tensor_scalar(out=neglam[:, :], in0=neglam[:, :], scalar1=-LAM_INIT, scalar2=None, op0=ALU.add),
         [B("neglam")], [B("neglam")])
    P.op("dve", lambda e: e.tensor_scalar(out=gsub[:, :], in0=gsub[:, :], scalar1=1.0 - LAM_INIT, scalar2=None, op0=ALU.mult),
         CB, [B("gsubs")])

    def R(name, n, shape, dt):
        return Rot(sB, name, n, shape, dt)
    WA = R("WA", 2, [128, 8, 448], BF16)
    WB = R("WB", 2, [128, 8, 192], BF16)
    WC = R("WC", 2, [128, 8, 384], BF16)
    KT_t = sb(sB, "KT", [128, NLB * 128], BF16)
    V_t = sb(sB, "V", [128, NLB, 130], BF16)
    P.op("pool", lambda e: e.memset(V_t[:, :, 128:130], 1.0), [], [B("Vones")])
    Kc_rot = R("Kc", 2, [128, 16, 128], BF16)
    Vc_rot = R("Vc", 2, [128, 16, 130], BF16)
    for t in Vc_rot.tiles:
        P.op("pool", lambda e, t=t: e.memset(t[:, :, 128:130], 1.0), [], [B("Vones")])
    KcT = sb(sB, "KcT", [128, PAST], BF16)
    s0_rot = R("s0h", 2, [64, 4, 128], F32)
    hTA_rot = R("hTA", 6, [128, 8, 128], BF16)
    hTC_rot = R("hTC", 3, [128, 8, 128], BF16)

    zA_r = R("zA", 3, [128, 448], F32)
    zB_r = R("zB", 2, [128, 192], F32)
    sqj = sb(sB, "sqj", [128, 8, 128], BF16)
    sqn = [0]

    def junk(w):
        n_ = sqn[0] % 8
        sqn[0] += 1
        return sqj[:, n_, 0:w], B("sqj", n_)
    ss4_r = R("ss4", 3, [128, 4], F32)
    ln4_r = R("ln4", 3, [128, 4], F32)
    rs4_r = R("rs4", 3, [128, 4], F32)
    knf_r = R("knf", 2, [128, 128], F32)
    knb_r = R("knb", 3, [128, 128], BF16)
    qnb_r = R("qnb", 2, [128, 128], BF16)
    rt1_r = R("rt1", 2, [128, 128], F32)
    rt2_r = R("rt2", 2, [128, 128], F32)
    rot_r = R("rot", 2, [128, 128], F32)
    kdec_r = R("kdec", 4, [128, 4, 64], BF16)
    rbf_r = R("rbf", 2, [128, 192], BF16)
    vr_r = R("vrb", 6, [128, 128], BF16)
    tr_r = R("trT", 3, [64, 3, 128], BF16)
    qds_r = R("qds", 1, [64, 4, 128], BF16)
    QT_r = R("QTz", 3, [128, 2, 128], BF16)
    AT_r = R("AT", 2, [128, 128], BF16)
    KV_r = R("KV", 8, [64, 128], F32)
    S_r = R("S", 4, [64, 128], F32)
    St_r = R("St", 2, [64, 128], F32)
    Sb_r = R("Sbf", 2, [64, 128], BF16)
    SbS = sb(sB, "SbS", [64, 4, 128], BF16)
    PT_r = R("PT", 3, [128, 512], BF16)
    PTd = sb(sB, "PTd", [128, 2, 128], BF16)
    oasb_r = R("oasb", 3, [128, 258], F32)
    orsb_r = R("orsb", 3, [128, 128], F32)
    mixb_r = R("mixb", 2, [128, 128], BF16)
    P.op("pool", lambda e: e.memset(PTd[:, :, :], 0.0), [], [B("PTd")])
    for t in qds_r.tiles:
        P.op("pool", lambda e, t=t: e.memset(t[:, :, :], 0.0), [], [B("qds", 0)])
    for ti, t in enumerate(QT_r.tiles):
        P.op("pool", lambda e, t=t: e.memset(t[:, :, :], 0.0), [], [B("QTz", ti)])
    cm = {}
    for nm, shape in [("rl", [128, 2]), ("r2l", [128, 1]), ("o1", [128, 128]), ("oa", [128, 128]),
                      ("ssn", [128, 2]), ("lnn", [128, 2]), ("rsn", [128, 2]), ("An", [128, 128]), ("Rn", [128, 128]),
                      ("gret", [128, 128]), ("E", [128, 384]), ("L", [128, 384]), ("Sg", [128, 384]),
                      ("T1", [128, 128]), ("T2", [128, 128]), ("T3", [128, 128]), ("T4", [128, 128])]:
        cm[nm] = sb(sB, "cm_" + nm, shape, F32)

    pA_by = [bank(0, 0, 448), bank(7, 0, 448)]
    pB = bank(1, 0, 192)
    pKV = bank(1, 192, 320)
    pAT = bank(1, 320, 448)
    pC = bank(2, 0, 384)
    pOr = bank(2, 384, 512)
    pOa = bank(3, 0, 258)
    pT = bank(4).bitcast(BF16)
    pS = [bank(5), bank(6)]
    tslot = [0]
    sset = [0]

    def transpose_to(src_ap, src_bufs, rows_out, dsts):
        s = tslot[0] % 8
        tslot[0] += 1
        pt = pT[0:rows_out, s * 128:(s + 1) * 128]
        with P.atomic():
            P.op("pe", lambda e: e.transpose(out=pt, in_=src_ap, identity=ident[:, :]), list(src_bufs) + [B("ident")], [B("pT", s)])
            for (dst_ap, sel, dst_bufs) in dsts:
                P.op("dve", lambda e, dst_ap=dst_ap, sel=sel: e.tensor_copy(out=dst_ap, in_=sel(pt)), [B("pT", s)], dst_bufs)

    H = {}

    def load_weights(h):
        wa, wab = WA.next()
        wb, wbb = WB.next()
        wc, wcb = WC.next()
        src = w_in[:, h, :].rearrange("(kt p) c -> p kt c", p=128)
        P.dma("pool", [(wa[:, 0:4, :], src[:, 0:4, 0:448]), (wa[:, 4:8, :], src[:, 4:8, 0:448])], "WA%d" % (h % 2), writes=[wab])
        P.dma("pool", [(wb[:, :, :], src[:, :, 448:640])], "WB%d" % (h % 2), writes=[wbb])
        P.dma("pool", [(wc[:, 0:4, :], src[:, 0:4, 640:1024]), (wc[:, 4:8, :], src[:, 4:8, 640:1024])], "WC%d" % (h % 2), writes=[wcb])
        H.setdefault(h, {})["w"] = (wa, wab, wb, wbb, wc, wcb)

    cslots = {}

    def load_cache(h, s):
        Kc, Kcb = Kc_rot.next()
        Vc, Vcb = Vc_rot.next()
        ksrc = ck[s, :, h * 128:(h + 1) * 128].rearrange("(b p) d -> p b d", p=128)
        vsrc = cv[s, :, h * 128:(h + 1) * 128].rearrange("(b p) d -> p b d", p=128)
        P.dma("pool", [(Kc[:, :, :], ksrc)], "Kc%d" % (Kc_rot.i % 2), writes=[Kcb])
        P.dma("pool", [(Vc[:, :, 0:128], vsrc)], "Vc%d" % (Vc_rot.i % 2), writes=[Vcb])
        cslots[(h, s)] = (Kc, Kcb, Vc, Vcb)

    heads = list(range(CFG["nheads"]))
    nit = CFG["niters"]
    its = list(range(nit)) + ([SAMP] if CFG["sample"] else [])
    seq = [(h, i) for h in heads for i in its]
    hTA = {}

    def fetch_hTA(h, i):
        lbs = [SAMP] if i == SAMP else [NOWN + i, i]
        for lb in lbs:
            t, tb = hTA_rot.next()
            P.dma("sp", [(t[:, :, :], hT_scr[lb].rearrange("p (k t) -> p k t", k=8))], "hTA%d" % (hTA_rot.i % 6),
                  reads=[B("hTs", lb)], writes=[tb])
            hTA[(h, lb)] = (t, tb)

    def KTb(lb):
        return B("KT", lb)

    def Vb(lb):
        return B("V", lb)

    def proj(h, lb, own, st):
        wa, wab, wb, wbb, wc, wcb = H[h]["w"]
        tok = slice(lb * 128, (lb + 1) * 128)
        samp = (lb == SAMP)
        hTt, hTtb = hTA[(h, lb)]
        zA, zAb = zA_r.next()
        ss4, ss4b = ss4_r.next()
        ln4, ln4b = ln4_r.next()
        rs4, rs4b = rs4_r.next()
        pA = pA_by[1 if own else 0]
        pAb = B("pA", 1 if own else 0)
        for kt in range(8):
            P.op("pe", lambda e, kt=kt: e.matmul(pA, lhsT=hTt[:, kt, :], rhs=wa[:, kt, :], start=(kt == 0), stop=(kt == 7)),
                 [hTtb, wab], [pAb])
        P.op("act", lambda e: e.activation(out=zA[:, :], in_=pA, func=AF.Copy), [pAb], [zAb])
        for g in range(2):
            jk, jkb = junk(64)
            P.op("act", lambda e, g=g, jk=jk: e.activation(out=jk, in_=pA[:, g * 64:(g + 1) * 64], func=AF.Square,
                                                          accum_out=ss4[:, g:g + 1]), [pAb], [jkb, B("ss4", ss4_r.i % 3, g)])
        zB, zBb = (None, None)
        if own:
            zB, zBb = zB_r.next()
            with P.atomic():
                for kt in range(8):
                    P.op("pe", lambda e, kt=kt: e.matmul(pB, lhsT=hTt[:, kt, :], rhs=wb[:, kt, :], start=(kt == 0), stop=(kt == 7)),
                         [hTtb, wbb], [B("pB")])
                P.op("act", lambda e: e.activation(out=zB[:, :], in_=pB, func=AF.Copy), [B("pB")], [zBb])
                for g in range(2):
                    jk, jkb = junk(64)
                    P.op("act", lambda e, g=g, jk=jk: e.activation(out=jk, in_=pB[:, g * 64:(g + 1) * 64], func=AF.Square,
                                                                  accum_out=ss4[:, 2 + g:3 + g]), [B("pB")], [jkb, B("ss4", ss4_r.i % 3, 2 + g)])
        ng = 4 if own else 2
        P.op("act", lambda e: e.activation(out=ln4[:, 0:ng], in_=ss4[:, 0:ng], func=AF.Ln, scale=1.0 / 64, bias=epsT[:, :]),
             [B("ss4", ss4_r.i % 3, g_) for g_ in range(ng)] + [B("eps")], [ln4b])
        P.op("act", lambda e: e.activation(out=rs4[:, 0:ng], in_=ln4[:, 0:ng], func=AF.Exp, scale=-0.5), [ln4b], [rs4b])
        P.op("pool", lambda e: e.tensor_copy(out=V_t[:, lb, 0:128], in_=zA[:, OV:OV + 128]), [zAb], [Vb(lb)])
        vrb, vrbb = vr_r.next()
        P.op("pool", lambda e: e.tensor_copy(out=vrb[:, :], in_=zA[:, OVR:OVR + 128]), [zAb], [vrbb])
        rt1, rt1b = rt1_r.next()
        rt2, rt2b = rt2_r.next()
        rot, rotb = rot_r.next()
        cos_bc = bass.AP(cosT, lb * 32, [[NLB * 32, 128], [0, 2], [1, 32]])
        sin_ap = sinT[:, lb, :]

        def rotary(src, src_bufs, c0):
            sw = bass.AP(src.tensor, src.offset + 32, [list(src.ap[0]), [-32, 2], [1, 32]])
            P.op("dve", lambda e: e.tensor_tensor(out=rt1[:, c0:c0 + 64].rearrange("p (a b) -> p a b", a=2),
                                                  in0=src.rearrange("p (a b) -> p a b", a=2), in1=cos_bc, op=ALU.mult),
                 src_bufs + [B("constB")], [B("rt1h", rt1_r.i % 2, c0)])
            P.op("pool", lambda e: e.tensor_tensor(out=rt2[:, c0:c0 + 64].rearrange("p (a b) -> p a b", a=2), in0=sw,
                                                   in1=sin_ap.rearrange("p (a b) -> p a b", a=2), op=ALU.mult),
                 src_bufs + [B("constB")], [B("rt2h", rt2_r.i % 2, c0)])
        rotary(zA[:, OKR:OKR + 64], [zAb], 0)
        if own:
            rotary(zB[:, 128:192], [zBb], 64)
        nr = 128 if own else 64
        P.op("dve", lambda e: e.tensor_tensor(out=rot[:, 0:nr], in0=rt1[:, 0:nr], in1=rt2[:, 0:nr], op=ALU.add),
             [B("rt1h", rt1_r.i % 2, c_) for c_ in ((0, 64) if own else (0,))] + [B("rt2h", rt2_r.i % 2, c_) for c_ in ((0, 64) if own else (0,))], [rotb])
        knb, knbb = knb_r.next()
        if own:
            knf, knfb = knf_r.next()
            for g in range(2):
                P.op("dve", lambda e, g=g: e.scalar_tensor_tensor(out=knf[:, g * 64:(g + 1) * 64], in0=zA[:, g * 64:(g + 1) * 64],
                                                                  scalar=rs4[:, g:g + 1], in1=gk[:, :], op0=ALU.mult, op1=ALU.mult),
                     [zAb, rs4b, B("constB")], [B("knfh", knf_r.i % 2, g)])
            knfh = [B("knfh", knf_r.i % 2, g) for g in range(2)]
            P.dma("sp", [(k_own[tok, h * 128:(h + 1) * 128], knf[:, :])], "ko%d" % (knf_r.i % 2), reads=knfh, final=True)
            knbh = [B("knbh", knb_r.i % 3, g) for g in range(2)]
            P.op("pool", lambda e: e.tensor_copy(out=knb[:, :], in_=knf[:, :]), knfh, knbh)
            P.dma("sp", [(v_own[tok, h * 128:(h + 1) * 128], zA[:, OV:OV + 128])], "vo%d" % (zA_r.i % 3), reads=[zAb], final=True)
        else:
            for g in range(2):
                P.op("dve", lambda e, g=g: e.scalar_tensor_tensor(out=knb[:, g * 64:(g + 1) * 64], in0=zA[:, g * 64:(g + 1) * 64],
                                                                  scalar=rs4[:, g:g + 1], in1=gk[:, :], op0=ALU.mult, op1=ALU.mult),
                     [zAb, rs4b, B("constB")], [B("knbh", knb_r.i % 3, g)])
            knbh = [B("knbh", knb_r.i % 3, g) for g in range(2)]
        transpose_to(knb[:, :], knbh, 128, [(KT_t[:, tok], lambda pt: pt, [KTb(lb)])])
        kdec, kdecb = kdec_r.next()
        kvs = []
        nstream = 4 if samp else 1
        for s in range(nstream):
            sc = kdcs[:, h, s:s + 1] if samp else kdc[:, h:h + 1]
            P.op("pool", lambda e, s=s, sc=sc: e.tensor_scalar(out=kdec[:, s, :], in0=rot[:, 0:64], scalar1=sc, scalar2=0.0,
                                                                op0=ALU.mult, op1=ALU.add), [rotb, B("constB")], [kdecb])
        for s in range(nstream):
            KV, KVb = KV_r.next()
            with P.atomic():
                P.op("pe", lambda e, s=s: e.matmul(pKV[0:64, :], lhsT=kdec[:, s, :], rhs=vrb[:, :], start=True, stop=True),
                     [kdecb, vrbb], [B("pKV")])
                P.op("dve", lambda e, KV=KV: e.tensor_copy(out=KV[:, :], in_=pKV[0:64, :]), [B("pKV")], [KVb])
            kvs.append((KV, KVb))
        st["kv_" + ("own" if own else "oth")] = kvs
        if not own:
            return
        st["vrb"] = (vrb, vrbb)
        qnb, qnbb = qnb_r.next()
        for g in range(2):
            P.op("dve", lambda e, g=g: e.scalar_tensor_tensor(out=qnb[:, g * 64:(g + 1) * 64], in0=zB[:, g * 64:(g + 1) * 64],
                                                              scalar=rs4[:, 2 + g:3 + g], in1=gq[:, :], op0=ALU.mult, op1=ALU.mult),
                 [zBb, rs4b, B("constB")], [B("qnbh", qnb_r.i % 2, g)])
        QT, QTb = QT_r.next()
        transpose_to(qnb[:, :], [B("qnbh", qnb_r.i % 2, g) for g in range(2)], 128, [(QT[0:64, 0, :], lambda pt: pt[0:64, :], [QTb]),
                                              (QT[64:128, 1, :], lambda pt: pt[64:128, :], [QTb])])
        st["QT"] = (QT, QTb)
        rbf, rbfb = rbf_r.next()
        P.op("pool", lambda e: e.tensor_copy(out=rbf[:, 0:128], in_=rot[:, 0:128]), [rotb], [rbfb])
        qd_ap = (qdcs if samp else qdc)[:, h:h + 1]
        P.op("pool", lambda e: e.tensor_scalar(out=rbf[:, 128:192], in0=rot[:, 64:128], scalar1=qd_ap, scalar2=0.0, op0=ALU.mult, op1=ALU.add),
             [rotb, B("constB")], [rbfb])
        trT, trTb_ = tr_r.next()
        ti = tr_r.i % 3
        for j in range(2):
            transpose_to(rbf[:, j * 64:(j + 1) * 64], [rbfb], 64, [(trT[:, j, :], lambda pt: pt, [B("trT", ti, j)])])
        if samp:
            qds, qdsb = qds_r.next()
            transpose_to(rbf[:, 128:192], [rbfb], 64,
                         [(qds[:, s, s * 32:(s + 1) * 32], (lambda pt, s=s: pt[:, s * 32:(s + 1) * 32]), [qdsb]) for s in range(4)])
            st["qds"] = (qds, qdsb)
        else:
            transpose_to(rbf[:, 128:192], [rbfb], 64, [(trT[:, 2, :], lambda pt: pt, [B("trT", ti, 2)])])
        st["trT"] = (trT, [B("trT", ti, j) for j in range(3)])

    def stA(h, i, st, nxt):
        if i == its[0]:
            s0h, s0hb = s0_rot.next()
            P.dma("sp", [(s0h[:, :, :], sr[:, h, :, :].rearrange("s d e -> d s e"))], "s0h%d" % (s0_rot.i % 2), writes=[s0hb])
            H[h]["s0"] = (s0h, s0hb)
            H[h]["S"] = None
        if nxt is not None:
            fetch_hTA(*nxt)
        if len(its) > 3 and i == its[3] and h + 1 < len(heads):
            load_weights(h + 1)
        if i == SAMP:
            proj(h, SAMP, True, st)
        else:
            proj(h, NOWN + i, False, st)

    def stA2(h, i, st):
        if i != SAMP:
            proj(h, i, True, st)

    def chain(h, a_col, b_col, KV, KVb):
        S_new, S_newb = S_r.next()
        cur = H[h]["S"]
        b_ap = chn[:, b_col:b_col + 1]
        if cur is None:
            P.op("dve", lambda e: e.tensor_scalar(out=S_new[:, :], in0=KV[:, :], scalar1=b_ap, scalar2=None, op0=ALU.mult),
                 [KVb, B("constB")], [S_newb])
        else:
            So, Sob = cur
            St, Stb = St_r.next()
            a_ap = chn[:, a_col:a_col + 1]
            P.op("pool", lambda e: e.tensor_scalar(out=St[:, :], in0=So[:, :], scalar1=a_ap, scalar2=0.0, op0=ALU.mult, op1=ALU.add),
                 [Sob, B("constB")], [Stb])
            P.op("dve", lambda e: e.scalar_tensor_tensor(out=S_new[:, :], in0=KV[:, :], scalar=b_ap, in1=St[:, :], op0=ALU.mult, op1=ALU.add),
                 [KVb, Stb, B("constB")], [S_newb])
        H[h]["S"] = (S_new, S_newb)

    def pv(first, PT_ap_fn, ptb, lbs, Vsrc, last):
        for kbi, lb in enumerate(lbs):
            for m in range(2):
                stt = first[0]
                first[0] = False
                fin = last and kbi == len(lbs) - 1 and m == 1
                P.op("pe", lambda e, kbi=kbi, lb=lb, m=m, stt=stt, fin=fin: e.matmul(
                    pOa[:, m * 129:(m + 1) * 129], lhsT=PT_ap_fn(kbi, m), rhs=V_t[:, lb, 0:129], start=stt, stop=fin,
                    skip_group_check=True), [ptb, Vb(lb), B("Vones")], [B("pOa")])

    def prompt_attention(h, i, QT, QTb):
        keys = []
        for j in range(i):
            keys.append(j)
            keys.append(NOWN + j)
        first = [True]
        groups = [keys[g:g + 2] for g in range(0, len(keys), 2)] + [[NOWN + i, i]]
        G = len(groups)
        info = [None] * G
        QT2 = QT[:, :, :].rearrange("p m q -> p (m q)")

        def S(g):
            set_ = sset[0] % 2
            sset[0] += 1
            PT, PTb = PT_r.next()
            info[g] = (set_, PT, PTb)
            for kbi, lb in enumerate(groups[g]):
                P.op("pe", lambda e, kbi=kbi, lb=lb: e.matmul(
                    pS[set_][:, kbi * 256:(kbi + 1) * 256], lhsT=KT_t[:, lb * 128:(lb + 1) * 128], rhs=QT2, start=True, stop=True),
                    [KTb(lb), QTb], [B("pS", set_)])

        def E(g):
            set_, PT, PTb = info[g]
            if g < G - 1:
                n = len(groups[g])
                P.op("act", lambda e: e.activation(out=PT[:, 0:n * 256], in_=pS[set_][:, 0:n * 256], func=AF.Exp, scale=0.125),
                     [B("pS", set_)], [PTb])
            else:
                P.op("act", lambda e: e.activation(out=PT[:, 0:256], in_=pS[set_][:, 0:256], func=AF.Exp, scale=0.125, bias=obias[:, :]),
                     [B("pS", set_), B("constB")], [PTb])
                P.op("act", lambda e: e.activation(out=PTd[0:64, :, :], in_=pS[set_][0:64, 256:512].rearrange("p (m c) -> p m c", m=2),
                                                   func=AF.Exp, scale=0.125), [B("pS", set_)], [B("PTd")])
                P.op("act", lambda e: e.activation(out=PTd[64:128, :, 64:128],
                                                   in_=pS[set_][64:128, 256:512].rearrange("p (m c) -> p m c", m=2)[:, :, 64:128],
                                                   func=AF.Exp, scale=0.125), [B("pS", set_)], [B("PTd")])

        def Pv(g):
            set_, PT, PTb = info[g]
            if g < G - 1:
                pv(first, lambda kbi, m: PT[:, kbi * 256 + m * 128: kbi * 256 + (m + 1) * 128], PTb, groups[g], None, False)
            else:
                pv(first, lambda kbi, m: PT[:, m * 128:(m + 1) * 128], PTb, [NOWN + i], None, False)
                pv(first, lambda kbi, m: PTd[:, m, :], B("PTd"), [i], None, True)

        for g in range(G + 2):
            if g < G:
                S(g)
            if 1 <= g <= G:
                E(g - 1)
            if g >= 2:
                Pv(g - 2)

    def sample_stream(h, s, QT, QTb):
        first = [True]
        Kc, Kcb, Vc, Vcb = cslots[(h, s)]
        for b4 in range(4):
            tslot[0] = (tslot[0] + 3) // 4 * 4
            s0 = tslot[0] % 8
            tslot[0] += 4
            with P.atomic():
                for j in range(4):
                    bb = b4 * 4 + j
                    P.op("pe", lambda e, bb=bb, j=j, s0=s0: e.transpose(out=pT[:, (s0 + j) * 128:(s0 + j + 1) * 128], in_=Kc[:, bb, :], identity=ident[:, :]),
                         [Kcb, B("ident")], [B("pT", s0 + j)])
                P.op("dve", lambda e, b4=b4, s0=s0: e.tensor_copy(out=KcT[:, b4 * 512:(b4 + 1) * 512], in_=pT[:, s0 * 128:(s0 + 4) * 128]),
                     [B("pT", s0 + j) for j in range(4)], [B("KcT", b4)])
        q_sl = slice(s * 32, (s + 1) * 32)
        QTs = QT[:, :, q_sl]
        info = [None] * 3

        def S(half):
            set_ = sset[0] % 2
            sset[0] += 1
            PT, PTb = PT_r.next()
            info[half] = (set_, PT, PTb)
            if half < 2:
                for kbi in range(8):
                    b = half * 8 + kbi
                    P.op("pe", lambda e, kbi=kbi, b=b: e.matmul(
                        pS[set_][:, kbi * 64:(kbi + 1) * 64], lhsT=KcT[:, b * 128:(b + 1) * 128], rhs=QTs, start=True, stop=True),
                        [B("KcT", b // 4), QTb], [B("pS", set_)])
            else:
                P.op("pe", lambda e: e.matmul(pS[set_][:, 0:64], lhsT=KT_t[:, SAMP * 128:(SAMP + 1) * 128], rhs=QTs, start=True, stop=True),
                     [KTb(SAMP), QTb], [B("pS", set_)])

        def E(half):
            set_, PT, PTb = info[half]
            if half < 2:
                P.op("act", lambda e: e.activation(out=PT[:, :], in_=pS[set_], func=AF.Exp, scale=0.125), [B("pS", set_)], [PTb])
            else:
                P.op("act", lambda e: e.activation(out=PT[:, 0:64], in_=pS[set_][:, 0:64], func=AF.Exp, scale=0.125, bias=sbias[:, s:s + 1]),
                     [B("pS", set_), B("constB")], [PTb])

        def Pv(half):
            set_, PT, PTb = info[half]
            if half < 2:
                for kbi in range(8):
                    b = half * 8 + kbi
                    for m in range(2):
                        stt = first[0]
                        first[0] = False
                        P.op("pe", lambda e, kbi=kbi, b=b, m=m, stt=stt: e.matmul(
                            pOa[s * 32:(s + 1) * 32, m * 129:(m + 1) * 129], lhsT=PT[:, kbi * 64 + m * 32: kbi * 64 + (m + 1) * 32],
                            rhs=Vc[:, b, 0:129], start=stt, stop=False, skip_group_check=True, tile_position=(0, s * 32)),
                            [PTb, Vcb, B("Vones")], [B("pOa")])
            else:
                for m in range(2):
                    P.op("pe", lambda e, m=m: e.matmul(
                        pOa[s * 32:(s + 1) * 32, m * 129:(m + 1) * 129], lhsT=PT[:, m * 32:(m + 1) * 32],
                        rhs=V_t[:, SAMP, 0:129], start=False, stop=(m == 1), skip_group_check=True, tile_position=(0, s * 32)),
                        [PTb, Vb(SAMP), B("Vones")], [B("pOa")])

        S(0)
        S(1)
        E(0)
        S(2)
        E(1)
        Pv(0)
        E(2)
        Pv(1)
        Pv(2)
        if s + 2 < 4:
            load_cache(h, s + 2)

    def stB(h, i, st):
        wa, wab, wb, wbb, wc, wcb = H[h]["w"]
        samp = (i == SAMP)
        lb = i
        t, tb = hTC_rot.next()
        P.dma("sp", [(t[:, :, :], hT_scr[lb].rearrange("p (k t) -> p k t", k=8))], "hTC%d" % (hTC_rot.i % 3),
              reads=[B("hTs", lb)], writes=[tb])
        st["hTC"] = (t, tb)
        if (not samp) and CFG["sample"] and i == its[-2]:
            load_cache(h, 0)
            load_cache(h, 1)
        trT, trb = st["trT"]
        vrb, vrbb = st["vrb"]
        QT, QTb = st["QT"]
        P.op("pe", lambda e: e.matmul(pAT, lhsT=trT[:, 0, :], rhs=trT[:, 1, :], start=True, stop=True), [trb[0], trb[1]], [B("pAT")])
        AT, ATb = AT_r.next()
        dtab = (DTs if samp else DT)[:, h, :]
        P.op("dve", lambda e: e.tensor_tensor(out=AT[:, :], in0=pAT, in1=dtab, op=ALU.mult), [B("pAT"), B("constB")], [ATb])
        orsb, orsbb = orsb_r.next()
        if samp:
            s0h, s0hb = H[h]["s0"]
            qds, qdsb = st["qds"]
            P.op("pool", lambda e: e.tensor_copy(out=SbS[:, :, :], in_=s0h[:, :, :]), [s0hb], [B("SbS")])
            with P.atomic():
                for s in range(4):
                    P.op("pe", lambda e, s=s: e.matmul(pOr, lhsT=qds[:, s, :], rhs=SbS[:, s, :], start=(s == 0), stop=False),
                         [qdsb, B("SbS")], [B("pOr")])
                P.op("pe", lambda e: e.matmul(pOr, lhsT=AT[:, :], rhs=vrb[:, :], start=False, stop=True), [ATb, vrbb], [B("pOr")])
            for s in range(4):
                KV, KVb = st["kv_own"][s]
                So_, Sob_ = S_r.next()
                P.op("dve", lambda e, s=s, So_=So_, KV=KV: e.scalar_tensor_tensor(out=So_[:, :], in0=s0h[:, s, :], scalar=chn[:, 40 + h:41 + h],
                                                                                  in1=KV[:, :], op0=ALU.mult, op1=ALU.add),
                     [KVb, B("constB"), s0hb], [Sob_])
                P.dma("sp", [(ret_s[s, h, :, :], So_[:, :])], "rs%d" % (S_r.i % 4), reads=[Sob_], final=True)
        else:
            KVo, KVob = st["kv_oth"][0]
            KVn, KVnb = st["kv_own"][0]
            chain(h, 0 + h, 8 + h, KVo, KVob)
            Sa, Sab = H[h]["S"]
            Sbf, Sbfb = Sb_r.next()
            P.op("pool", lambda e: e.tensor_copy(out=Sbf[:, :], in_=Sa[:, :]), [Sab], [Sbfb])
            with P.atomic():
                P.op("pe", lambda e: e.matmul(pOr, lhsT=AT[:, :], rhs=vrb[:, :], start=True, stop=False), [ATb, vrbb], [B("pOr")])
                P.op("pe", lambda e: e.matmul(pOr, lhsT=trT[:, 2, :], rhs=Sbf[:, :], start=False, stop=True), [trb[2], Sbfb], [B("pOr")])
            S_new, S_newb = S_r.next()
            P.op("dve", lambda e: e.scalar_tensor_tensor(out=S_new[:, :], in0=Sa[:, :], scalar=chn[:, 32 + h:33 + h], in1=KVn[:, :],
                                                         op0=ALU.mult, op1=ALU.add), [Sab, KVnb, B("constB")], [S_newb])
            H[h]["S"] = (S_new, S_newb)
            chain(h, 16 + h, 24 + h, KVo, KVob)
            if i == its[-1] or (CFG["sample"] and i == its[-2]):
                Sf, Sfb = H[h]["S"]
                P.dma("sp", [(ret_p[h, :, :], Sf[:, :])], "rp", reads=[Sfb], final=True)
        P.op("act", lambda e: e.activation(out=orsb[:, :], in_=pOr, func=AF.Copy), [B("pOr")], [orsbb])
        st["orsb"] = (orsb, orsbb)

    def stB2(h, i, st):
        samp = (i == SAMP)
        QT, QTb = st["QT"]
        if samp:
            for s in range(4):
                sample_stream(h, s, QT, QTb)
        else:
            prompt_attention(h, i, QT, QTb)
        oasb, oasbb = oasb_r.next()
        P.op("dve", lambda e: e.tensor_copy(out=oasb[:, :], in_=pOa), [B("pOa")], [oasbb])
        st["oasb"] = (oasb, oasbb)

    def stC(h, i, st):
        wa, wab, wb, wbb, wc, wcb = H[h]["w"]
        lb = i
        tok = slice(lb * 128, (lb + 1) * 128)
        c = cm
        hTt, hTtb = st["hTC"]
        oasb, oasbb = st["oasb"]
        orsb, orsbb = st["orsb"]
        with P.atomic():
            for kt in range(8):
                P.op("pe", lambda e, kt=kt: e.matmul(pC, lhsT=hTt[:, kt, :], rhs=wc[:, kt, :], start=(kt == 0), stop=(kt == 7)),
                     [hTtb, wcb], [B("pC")])
        P.op("act", lambda e: e.activation(out=c["E"][:, :], in_=pC, func=AF.Exp, scale=-1.0), [B("pC")], [B("cm_E")])
        P.op("act", lambda e: e.activation(out=c["gret"][:, :], in_=pC[:, 0:128], func=AF.Copy), [B("pC")], [B("cm_gret")])
        P.op("act", lambda e: e.activation(out=c["L"][:, :], in_=c["E"][:, :], func=AF.Ln, bias=oneT[:, :]), [B("cm_E"), B("oneT")], [B("cm_L")])
        P.op("act", lambda e: e.activation(out=c["Sg"][:, :], in_=c["L"][:, :], func=AF.Exp, scale=-1.0), [B("cm_L")], [B("cm_Sg")])
        P.op("dve", lambda e: e.reciprocal(out=c["rl"][:, :], in_=bass.AP(oasb, 128, [[258, 128], [129, 2]])), [oasbb], [B("cm_rl")])
        P.op("dve", lambda e: e.tensor_tensor(out=c["r2l"][:, :], in0=c["rl"][:, 1:2], in1=neglam[:, :], op=ALU.mult),
             [B("cm_rl"), B("neglam")], [B("cm_r2l")])
        P.op("dve", lambda e: e.tensor_scalar(out=c["o1"][:, :], in0=oasb[:, 0:128], scalar1=c["rl"][:, 0:1], scalar2=None, op0=ALU.mult),
             [oasbb, B("cm_rl")], [B("cm_o1")])
        P.op("dve", lambda e: e.scalar_tensor_tensor(out=c["oa"][:, :], in0=oasb[:, 129:257], scalar=c["r2l"][:, :], in1=c["o1"][:, :],
                                                     op0=ALU.mult, op1=ALU.add), [oasbb, B("cm_r2l"), B("cm_o1")], [B("cm_oa")])
        jk1, jk1b = junk(128)
        jk2, jk2b = junk(128)
        P.op("act", lambda e: e.activation(out=jk1, in_=c["oa"][:, :], func=AF.Square, accum_out=c["ssn"][:, 0:1]),
             [B("cm_oa")], [jk1b, B("cm_ssn0")])
        P.op("act", lambda e: e.activation(out=jk2, in_=orsb[:, :], func=AF.Square, accum_out=c["ssn"][:, 1:2]),
             [orsbb], [jk2b, B("cm_ssn1")])
        P.op("act", lambda e: e.activation(out=c["lnn"][:, :], in_=c["ssn"][:, :], func=AF.Ln, scale=1.0 / 128, bias=epsT[:, :]),
             [B("cm_ssn0"), B("cm_ssn1"), B("eps")], [B("cm_lnn")])
        P.op("act", lambda e: e.activation(out=c["rsn"][:, :], in_=c["lnn"][:, :], func=AF.Exp, scale=-0.5), [B("cm_lnn")], [B("cm_rsn")])
        P.op("dve", lambda e: e.scalar_tensor_tensor(out=c["An"][:, :], in0=c["oa"][:, :], scalar=c["rsn"][:, 0:1], in1=gsub[:, :],
                                                     op0=ALU.mult, op1=ALU.mult), [B("cm_oa"), B("cm_rsn"), B("gsubs")], [B("cm_An")])
        P.op("dve", lambda e: e.scalar_tensor_tensor(out=c["Rn"][:, :], in0=orsb[:, :], scalar=c["rsn"][:, 1:2], in1=grn[:, h * 128:(h + 1) * 128],
                                                     op0=ALU.mult, op1=ALU.mult), [orsbb, B("cm_rsn"), B("constB")], [B("cm_Rn")])
        P.op("pool", lambda e: e.tensor_tensor(out=c["T1"][:, :], in0=c["An"][:, :], in1=c["Sg"][:, 128:256], op=ALU.mult),
             [B("cm_An"), B("cm_Sg")], [B("cm_T1")])
        P.op("pool", lambda e: e.tensor_tensor(out=c["T2"][:, :], in0=c["gret"][:, :], in1=c["Sg"][:, 0:128], op=ALU.mult),
             [B("cm_gret"), B("cm_Sg")], [B("cm_T2")])
        P.op("dve", lambda e: e.tensor_tensor(out=c["T3"][:, :], in0=c["Rn"][:, :], in1=c["T2"][:, :], op=ALU.mult),
             [B("cm_Rn"), B("cm_T2")], [B("cm_T3")])
        P.op("pool", lambda e: e.tensor_tensor(out=c["T4"][:, :], in0=c["T3"][:, :], in1=c["Sg"][:, 256:384], op=ALU.mult),
             [B("cm_T3"), B("cm_Sg")], [B("cm_T4")])
        mixb, mixbb = mixb_r.next()
        P.op("dve", lambda e: e.tensor_tensor(out=mixb[:, :], in0=c["T1"][:, :], in1=c["T4"][:, :], op=ALU.add),
             [B("cm_T1"), B("cm_T4")], [mixbb])
        transpose_to(mixb[:, :], [mixbb], 128, [(mixT[:, h, tok], lambda pt: pt, [B("mixT", lb)])])

    load_weights(heads[0])
    fetch_hTA(*seq[0])
    states = [dict() for _ in seq]
    n = len(seq)
    ent = {}

    def cap(kind, t, fn, *args):
        P.begin()
        fn(*args)
        ent[(kind, t)] = dict(ops=P.end(), pos=0)

    for step in range(n + 2):
        if step < n:
            cap("A", step, stA, seq[step][0], seq[step][1], states[step], seq[step + 1] if step + 1 < n else None)
            cap("A2", step, stA2, seq[step][0], seq[step][1], states[step])
        if 1 <= step <= n:
            t = step - 1
            cap("B", t, stB, seq[t][0], seq[t][1], states[t])
            cap("B2", t, stB2, seq[t][0], seq[t][1], states[t])
        if step >= 2:
            t = step - 2
            cap("C", t, stC, seq[t][0], seq[t][1], states[t])

    def gates(kind, t):
        if kind == "A":
            return [("A", t - 1), ("B", t - 2), ("B2", t - 2)]
        if kind == "A2":
            return [("A2", t - 1), ("A", t - 1), ("B", t - 2), ("B2", t - 2)]
        if kind == "B":
            return [("A", t), ("A2", t), ("B", t - 1), ("C", t - 3)]
        if kind == "B2":
            return [("A", t), ("A2", t), ("B2", t - 1), ("C", t - 3)]
        return [("B", t), ("B2", t), ("C", t - 1)]

    def finished(key):
        e = ent.get(key)
        return e is None or e["pos"] >= len(e["ops"])

    pending = sorted(ent.keys(), key=lambda k: (k[1], k[0]))
    active = []
    while pending or active:
        still = []
        for k in pending:
            if all(finished(g) for g in gates(*k)):
                if ent[k]["ops"]:
                    active.append(k)
            else:
                still.append(k)
        pending = still
        if not active:
            assert not pending, "scheduler gate deadlock"
            break
        ratios = {k: ent[k]["pos"] / len(ent[k]["ops"]) for k in active}
        rmin = min(ratios.values())
        est = {k: P._peek(ent[k]["ops"][ent[k]["pos"]][0]) for k in active}
        tmin = min(est.values())
        age = min(k[1] for k in active)
        bk = min(active, key=lambda k: (est[k] - tmin) + 4000.0 * (ratios[k] - rmin) + 0.0 * (k[1] - age))
        e = ent[bk]
        for (kind_, args, kw) in e["ops"][e["pos"]]:
            if kind_ == "op":
                P.op(*args, **kw)
            else:
                P.dma(*args, **kw)
        e["pos"] += 1
        if e["pos"] >= len(e["ops"]):
            active.remove(bk)
    dbg("mixT", mixT[:, :, :], [128, NH, NOWN * 128], BF16, [B("mixT", lb) for lb in range(NOWN)])


def phase_C(nc, P, B, sC, sb, Rot, bank, dbg, env):
    ident = env["ident"]
    epsT = env["epsT"]
    mixT = env["mixT"]
    gvec = env["gvec"]
    x_own, p_own, y_own = env["x_own"], env["p_own"], env["y_own"]
    psum = env["psum"]
    xt_rot = Rot(sC, "xtc", 2, [128, D], F32)
    hb_rot = Rot(sC, "hbc", 2, [128, D], BF16)
    junk = sb(sC, "junkc", [128, D], BF16)
    groups = [list(range(0, 9)), list(range(9, 17))]
    TMAX = 9 * 128
    fgroups = [list(range(0, 4)), list(range(4, 8)), list(range(8, 12)), list(range(12, 16)), list(range(16, 19)), list(range(19, 22))]

    Wo = sb(sC, "Wo", [128, 8, D], BF16)
    Wpg = Wo
    Wple = sb(sC, "Wple", [128, 2, D], BF16)
    x1 = sb(sC, "x1", [128, 9, D], F32)
    hT2 = sb(sC, "hT2", [128, 8, TMAX], BF16)
    plT = sb(sC, "plT", [128, 2, TMAX], BF16)
    act_r = Rot(sC, "actT", 2, [128, 4, TMAX], BF16)
    Wd_r = Rot(sC, "Wd", 2, [128, 4, D], BF16)
    Wg_r = Rot(sC, "Wg", 2, [128, 8, 256], BF16)
    Wu_r = Rot(sC, "Wu", 2, [128, 8, 256], BF16)
    ssC = sb(sC, "ssC", [128, 9], F32)
    lnC = sb(sC, "lnC", [128, 9], F32)
    rsC = sb(sC, "rsC", [128, 9], F32)
    sg_r = Rot(sC, "sg", 2, [128, 512], BF16)
    pb_r = Rot(sC, "pb", 2, [128, PLE], BF16)
    sgm_r = Rot(sC, "sgm", 2, [128, D], F32)

    def load_sq(src):
        v = src.rearrange("(kt p) n -> p kt n", p=128)
        P.dma("pool", [(Wo[:, 0:4, :], v[:, 0:4, :]), (Wo[:, 4:8, :], v[:, 4:8, :])], "Wo", writes=[B("Wo")])
    P.dma("pool", [(Wple[:, :, :], env["w_ple"].rearrange("(kt p) n -> p kt n", p=128))], "Wple", writes=[B("Wple")])

    pT8 = bank(6).bitcast(BF16)
    w_g = env["w_g"].rearrange("(kt p) f -> p kt f", p=128)
    w_u = env["w_u"].rearrange("(kt p) f -> p kt f", p=128)
    w_d = env["w_d"].rearrange("(f p) n -> p f n", p=128)

    def norm_T(grp, gl, src_ap_fn, src_bufs_fn, ssb):
        for bi, lb in enumerate(grp):
            hb, hbb = hb_rot.next()
            P.op("dve", lambda e, bi=bi, hb=hb: e.scalar_tensor_tensor(out=hb[:, :], in0=x1[:, bi, :], scalar=rsC[:, bi:bi + 1], in1=gvec[:, :],
                                                                       op0=ALU.mult, op1=ALU.mult),
                 [B("x1", bi), ssb, B("gvec")], [hbb])
            for kt in range(8):
                P.op("pe", lambda e, hb=hb, kt=kt: e.transpose(out=pT8[:, kt * 128:(kt + 1) * 128], in_=hb[:, kt * 128:(kt + 1) * 128],
                                                                identity=ident[:, :]), [hbb, B("ident")], [B("bank", 6)])
            P.op("act", lambda e, bi=bi: e.activation(out=hT2[:, :, bi * 128:(bi + 1) * 128], in_=pT8.rearrange("p (k t) -> p k t", k=8),
                                                     func=AF.Copy), [B("bank", 6)], [B("hT2", bi)])

    def stats(grp, tag):
        for bi, lb in enumerate(grp):
            P.op("act", lambda e, bi=bi: e.activation(out=junk[:, :], in_=x1[:, bi, :], func=AF.Square, accum_out=ssC[:, bi:bi + 1]),
                 [B("x1", bi)], [B("junk"), B("ssC")])
        n = len(grp)
        P.op("act", lambda e: e.activation(out=lnC[:, 0:n], in_=ssC[:, 0:n], func=AF.Ln, scale=1.0 / D, bias=epsT[:, :]),
             [B("ssC"), B("eps")], [B("lnC")])
        P.op("act", lambda e: e.activation(out=rsC[:, 0:n], in_=lnC[:, 0:n], func=AF.Exp, scale=-0.5), [B("lnC")], [B("rsC")])

    for gi, grp in enumerate(groups):
        T = len(grp) * 128
        load_sq(env["w_o"])
        P.dma("sp", [(gvec[:, :], bc_rows(env["g_ffn"].tensor, 0, D))], "c1", writes=[B("gvec")])
        for bi, lb in enumerate(grp):
            tok = slice(lb * 128, (lb + 1) * 128)
            xt, xb = xt_rot.next()
            P.dma("sp", [(xt[:, :], x_own[tok, :])], "xtc%d" % (xt_rot.i % 2), writes=[xb])
            pb, pbb = pb_r.next()
            P.dma("pool", [(pb[:, :], p_own[tok, :])], "pb%d" % (pb_r.i % 2), writes=[pbb])
            for k2 in range(2):
                P.op("pe", lambda e, pb=pb, k2=k2: e.transpose(out=pT8[:, k2 * 128:(k2 + 1) * 128], in_=pb[:, k2 * 128:(k2 + 1) * 128],
                                                                identity=ident[:, :]), [pbb, B("ident")], [B("bank", 6)])
            P.op("act", lambda e, bi=bi: e.activation(out=plT[:, :, bi * 128:(bi + 1) * 128],
                                                     in_=pT8[:, 0:256].rearrange("p (k t) -> p k t", k=2), func=AF.Copy),
                 [B("bank", 6)], [B("plT", bi)])
            b0 = 2 * (bi % 2)
            for n in range(2):
                for hh in range(8):
                    P.op("pe", lambda e, n=n, hh=hh, tok=tok, b0=b0: e.matmul(bank(b0 + n), lhsT=mixT[:, hh, tok], rhs=Wo[:, hh, n * 512:(n + 1) * 512],
                                                                               start=(hh == 0), stop=(hh == 7)),
                         [B("mixT", lb), B("Wo")], [B("bank", b0 + n)])
            P.op("dve", lambda e, bi=bi, xt=xt, b0=b0: e.tensor_tensor(out=x1[:, bi, :], in0=psum[:, b0 * 512:(b0 + 2) * 512], in1=xt[:, :], op=ALU.add),
                 [B("bank", b0), B("bank", b0 + 1), xb], [B("x1", bi)])
        stats(grp, "ffn")
        norm_T(grp, gi, None, None, B("rsC"))
        for fg in fgroups:
            nf = len(fg)
            actT, actb = act_r.next()
            Wd, Wdb = Wd_r.next()
            P.dma("pool", [(Wd[:, 0:nf, :], w_d[:, fg[0]:fg[0] + nf, :])], "Wd%d" % (Wd_r.i % 2), writes=[Wdb])
            for f2 in range(0, nf, 2):
                f0 = fg[0] + f2
                n2 = min(2, nf - f2)
                Wg, Wgb = Wg_r.next()
                Wu, Wub = Wu_r.next()
                P.dma("pool", [(Wg[:, :, 0:n2 * 128], w_g[:, :, f0 * 128:(f0 + n2) * 128])], "Wg%d" % (Wg_r.i % 2), writes=[Wgb])
                P.dma("pool", [(Wu[:, :, 0:n2 * 128], w_u[:, :, f0 * 128:(f0 + n2) * 128])], "Wu%d" % (Wu_r.i % 2), writes=[Wub])
                for fi in range(n2):
                    fl = f2 + fi
                    for c0 in range(0, T, 512):
                        cw = min(512, T - c0)
                        pg = bank(2 + (c0 // 512) % 2, 0, cw)
                        pu = bank(4 + (c0 // 512) % 2, 0, cw)
                        gb_ = B("bank", 2 + (c0 // 512) % 2)
                        ub_ = B("bank", 4 + (c0 // 512) % 2)
                        tb = [B("hT2", bi) for bi in range(c0 // 128, (c0 + cw) // 128)]
                        for kt in range(8):
                            P.op("pe", lambda e, kt=kt, fi=fi, Wg=Wg, pg=pg, c0=c0, cw=cw: e.matmul(
                                pg, lhsT=Wg[:, kt, fi * 128:(fi + 1) * 128], rhs=hT2[:, kt, c0:c0 + cw], start=(kt == 0), stop=(kt == 7)),
                                tb + [Wgb], [gb_])
                        for kt in range(8):
                            P.op("pe", lambda e, kt=kt, fi=fi, Wu=Wu, pu=pu, c0=c0, cw=cw: e.matmul(
                                pu, lhsT=Wu[:, kt, fi * 128:(fi + 1) * 128], rhs=hT2[:, kt, c0:c0 + cw], start=(kt == 0), stop=(kt == 7)),
                                tb + [Wub], [ub_])
                        sg, sgb = sg_r.next()
                        P.op("act", lambda e, sg=sg, pg=pg, cw=cw: e.activation(out=sg[:, 0:cw], in_=pg, func=AF.Silu), [gb_], [sgb])
                        P.op("dve", lambda e, sg=sg, pu=pu, cw=cw, c0=c0, fl=fl, actT=actT: e.tensor_tensor(
                            out=actT[:, fl, c0:c0 + cw], in0=pu, in1=sg[:, 0:cw], op=ALU.mult), [ub_, sgb], [actb])
            for bi, lb in enumerate(grp):
                b0 = 6 * (bi % 2)
                for n in range(2):
                    for fl in range(nf):
                        P.op("pe", lambda e, n=n, fl=fl, bi=bi, actT=actT, Wd=Wd, nf=nf, b0=b0: e.matmul(
                            bank(b0 + n), lhsT=actT[:, fl, bi * 128:(bi + 1) * 128], rhs=Wd[:, fl, n * 512:(n + 1) * 512],
                            start=(fl == 0), stop=(fl == nf - 1)), [actb, Wdb], [B("bank", b0 + n)])
                P.op("dve", lambda e, bi=bi, b0=b0: e.tensor_tensor(out=x1[:, bi, :], in0=psum[:, b0 * 512:(b0 + 2) * 512], in1=x1[:, bi, :], op=ALU.add),
                     [B("bank", b0), B("bank", b0 + 1), B("x1", bi)], [B("x1", bi)])
        load_sq(env["w_pg"])
        P.dma("sp", [(gvec[:, :], bc_rows(env["g_ple"].tensor, 0, D))], "c1", writes=[B("gvec")])
        stats(grp, "ple")
        norm_T(grp, gi, None, None, B("rsC"))
        for bi, lb in enumerate(grp):
            tok = slice(lb * 128, (lb + 1) * 128)
            bz = 4 * (bi % 2)
            bp = 2 + 4 * (bi % 2)
            for n in range(2):
                for kt in range(8):
                    P.op("pe", lambda e, n=n, kt=kt, bi=bi, bz=bz: e.matmul(bank(bz + n), lhsT=hT2[:, kt, bi * 128:(bi + 1) * 128],
                                                                             rhs=Wpg[:, kt, n * 512:(n + 1) * 512], start=(kt == 0), stop=(kt == 7)),
                         [B("hT2", bi), B("Wo")], [B("bank", bz + n)])
            for n in range(2):
                for k2 in range(2):
                    P.op("pe", lambda e, n=n, k2=k2, bi=bi, bp=bp: e.matmul(bank(bp + n), lhsT=plT[:, k2, bi * 128:(bi + 1) * 128],
                                                                             rhs=Wple[:, k2, n * 512:(n + 1) * 512], start=(k2 == 0), stop=(k2 == 1)),
                         [B("plT", bi), B("Wple")], [B("bank", bp + n)])
            sgm, sgmb = sgm_r.next()
            P.op("act", lambda e, sgm=sgm, bz=bz: e.activation(out=sgm[:, :], in_=psum[:, bz * 512:(bz + 2) * 512], func=AF.Sigmoid),
                 [B("bank", bz), B("bank", bz + 1)], [sgmb])
            P.op("dve", lambda e, sgm=sgm, bp=bp: e.tensor_tensor(out=sgm[:, :], in0=psum[:, bp * 512:(bp + 2) * 512], in1=sgm[:, :], op=ALU.mult),
                 [B("bank", bp), B("bank", bp + 1), sgmb], [sgmb])
            P.op("pool", lambda e, sgm=sgm, bi=bi: e.tensor_tensor(out=sgm[:, :], in0=sgm[:, :], in1=x1[:, bi, :], op=ALU.add),
                 [sgmb, B("x1", bi)], [sgmb])
            P.dma("sp", [(y_own[tok, :], sgm[:, :])], "yo%d" % (sgm_r.i % 2), reads=[sgmb], final=True)


_PROG_CACHE = {}


def _consts(p):
    f32 = np.float32
    c = {}
    c["c_ident"] = np.eye(128, dtype=f32)
    r = np.arange(128)
    pos = np.zeros((NLB, 128), dtype=np.int64)
    for i in range(16):
        pos[i] = (2 * i + p) * 128 + r
        pos[NOWN + i] = (2 * i + 1 - p) * 128 + r
    pos[SAMP] = PAST + (r % 32)
    inv_freq = (np.float32(10000.0) ** (-np.arange(32, dtype=f32) / np.float32(32))).astype(f32)
    ang = pos.astype(f32)[:, :, None] * inv_freq[None, None, :]
    cos = np.cos(ang).astype(f32)
    sin = np.sin(ang).astype(f32)
    c["c_cos"] = np.ascontiguousarray(cos.transpose(1, 0, 2))
    c["c_sin"] = np.ascontiguousarray(np.concatenate([-sin, sin], axis=-1).transpose(1, 0, 2))
    log_g = np.log1p(-np.exp2(-5.0 - np.arange(NH, dtype=np.float64)))
    kscale = 64 ** -0.5
    j = r.astype(np.float64)
    c["c_kdc"] = (np.exp(log_g[None, :] * (127.0 - j[:, None])) * kscale).astype(f32)
    kd = np.zeros((128, NH, 4))
    for s in range(4):
        m = (r // 32 == s)
        kd[m, :, s] = np.exp(log_g[None, :] * (31.0 - (j[m] % 32)[:, None])) * kscale
    c["c_kdcs"] = kd.astype(f32)
    c["c_qdc"] = np.exp(log_g[None, :] * (j[:, None] + 1.0)).astype(f32)
    c["c_qdcs"] = np.exp(log_g[None, :] * ((j % 32)[:, None] + 1.0)).astype(f32)
    diff = j[None, :] - j[:, None]
    dt = np.where(diff[:, None, :] >= 0, np.exp(log_g[None, :, None] * np.maximum(diff[:, None, :], 0.0)), 0.0) * kscale
    c["c_dt"] = dt.astype(f32)
    same = (r[:, None] // 32 == r[None, :] // 32)
    c["c_dts"] = (dt * same[:, None, :]).astype(f32)
    chn = np.zeros((64, 48))
    g128 = np.exp(log_g * 128.0)
    g32 = np.exp(log_g * 32.0)
    chn[:, 0:8] = g128 if p == 1 else 1.0
    chn[:, 8:16] = 1.0 if p == 1 else 0.0
    chn[:, 16:24] = g128 if p == 0 else 1.0
    chn[:, 24:32] = 1.0 if p == 0 else 0.0
    chn[:, 32:40] = g128
    chn[:, 40:48] = g32
    c["c_chn"] = chn.astype(f32)
    c["c_obias"] = np.full((128, 1), 0.0 if p == 1 else -30000.0, dtype=f32)
    c["c_sbias"] = np.where(r[:, None] // 32 == np.arange(4)[None, :], 0.0, -30000.0).astype(f32)
    return c


def _perm_w_in(w_in):
    w = w_in
    segs = []
    for h in range(NH):
        cols = [w[:, 1024 + h * 128: 1024 + (h + 1) * 128],
                w[:, 3584 + h * 64: 3584 + (h + 1) * 64],
                w[:, 2048 + h * 128: 2048 + (h + 1) * 128],
                w[:, 4096 + h * 128: 4096 + (h + 1) * 128],
                w[:, 0 + h * 128: (h + 1) * 128],
                w[:, 3072 + h * 64: 3072 + (h + 1) * 64],
                w[:, 5120 + h * 128: 5120 + (h + 1) * 128],
                w[:, 6144 + h * 128: 6144 + (h + 1) * 128],
                w[:, 7168 + h * 128: 7168 + (h + 1) * 128]]
        segs.append(np.concatenate(cols, axis=1))
    return np.ascontiguousarray(np.stack(segs, axis=1))


def make_in_maps(inputs):
    f = lambda a: np.ascontiguousarray(np.asarray(a, dtype=np.float32))
    xp = f(inputs["x_prompt"]).reshape(4, 32, 128, D)
    xs = f(inputs["x_sample"])
    pp = f(inputs["p_prompt"])[0].reshape(4, 32, 128, PLE)
    psm = f(inputs["p_sample"])[0]
    ckf = f(inputs["cache_attn_k"])[0].reshape(32, PAST, D)
    cvf = f(inputs["cache_attn_v"])[0].reshape(32, PAST, D)
    srf = f(inputs["state_ret"])[0]
    shared = {
        "w_in": _perm_w_in(f(inputs["w_in"])[0]),
        "w_o": f(inputs["w_o"])[0], "w_g": f(inputs["w_ff_gate"])[0], "w_u": f(inputs["w_ff_up"])[0],
        "w_d": f(inputs["w_ff_down"])[0], "w_ple": f(inputs["w_ple"])[0], "w_pg": f(inputs["w_ple_gate"])[0],
        "g_mix": f(inputs["g_mix_norm"]).reshape(1, D), "g_ffn": f(inputs["g_ffn_norm"]).reshape(1, D),
        "g_ple": f(inputs["g_ple_norm"]).reshape(1, D), "g_q": f(inputs["g_q_norm"]).reshape(1, 64),
        "g_k": f(inputs["g_k_norm"]).reshape(1, 64), "g_sub": f(inputs["g_sub_norm"]).reshape(1, 128),
        "g_rn": f(inputs["g_ret_norm"]).reshape(1, NH * 128), "lam_q": f(inputs["lam_q"]).reshape(1, 128),
        "lam_k": f(inputs["lam_k"]).reshape(1, 128),
    }
    consts = [_consts(0), _consts(1)]
    maps = []
    for c in range(8):
        b, p = c // 2, c % 2
        own = list(range(p, 32, 2))
        oth = list(range(1 - p, 32, 2))
        m = dict(shared)
        m.update(consts[p])
        m["x_own"] = np.ascontiguousarray(np.concatenate([xp[b, own].reshape(2048, D), xs[4 * c:4 * c + 4].reshape(128, D)], axis=0))
        m["x_oth"] = np.ascontiguousarray(xp[b, oth].reshape(2048, D))
        m["p_own"] = np.ascontiguousarray(np.concatenate([pp[b, own].reshape(2048, PLE), psm[4 * c:4 * c + 4].reshape(128, PLE)], axis=0))
        m["ck"] = np.ascontiguousarray(ckf[4 * c:4 * c + 4])
        m["cv"] = np.ascontiguousarray(cvf[4 * c:4 * c + 4])
        m["sr"] = np.ascontiguousarray(srf[4 * c:4 * c + 4])
        maps.append(m)
    return maps


def assemble(results):
    y_p = np.zeros((4, 32, 128, D), np.float32)
    y_s = np.zeros((32, 32, D), np.float32)
    k_p = np.zeros((4, 32, 128, D), np.float32)
    v_p = np.zeros((4, 32, 128, D), np.float32)
    r_p = np.zeros((4, NH, 64, 128), np.float32)
    k_s = np.zeros((32, 32, D), np.float32)
    v_s = np.zeros((32, 32, D), np.float32)
    r_s = np.zeros((32, NH, 64, 128), np.float32)
    for c in range(8):
        b, p = c // 2, c % 2
        own = list(range(p, 32, 2))
        r = results[c]
        y_p[b, own] = r["y_own"][:2048].reshape(16, 128, D)
        k_p[b, own] = r["k_own"][:2048].reshape(16, 128, D)
        v_p[b, own] = r["v_own"][:2048].reshape(16, 128, D)
        y_s[4 * c:4 * c + 4] = r["y_own"][2048:].reshape(4, 32, D)
        k_s[4 * c:4 * c + 4] = r["k_own"][2048:].reshape(4, 32, D)
        v_s[4 * c:4 * c + 4] = r["v_own"][2048:].reshape(4, 32, D)
        r_s[4 * c:4 * c + 4] = r["ret_s"]
        if p == 1:
            r_p[b] = r["ret_p"]
    return (y_p.reshape(4, SEQ, D), y_s, k_p.reshape(1, 4, SEQ, NH, 128), v_p.reshape(1, 4, SEQ, NH, 128),
            r_p.reshape(1, 4, NH, 64, 128), k_s.reshape(1, 32, 32, NH, 128), v_s.reshape(1, 32, 32, NH, 128),
            r_s.reshape(1, 32, NH, 64, 128))


def kernel(**inputs):
    if "nc" not in _PROG_CACHE:
        _PROG_CACHE["nc"] = build_program()[0]
    nc = _PROG_CACHE["nc"]
    maps = make_in_maps(inputs)
    res = run_bass_kernel_spmd(nc, maps, core_ids=list(range(8)))
    return assemble(res.results)
```
